# Optimizing a Trainium2 kernel written in Bass

```python
import math
import jax, jax.numpy as jnp
from jax import lax
import numpy as np

D_MODEL = 2048
BATCH = 4
SEQ = 2048
DEPTH = 4

GRID_W = 64
CTX_LEN = 256
N_MIXERS = 2
N_RET_LAYERS = (DEPTH + 1) // 2
N_ATTN_LAYERS = DEPTH // 2
RET_HEADS = 8
RET_DK = D_MODEL // RET_HEADS
RET_DV = 2 * RET_DK
RET_CHUNK = 128
RET_QK_W = RET_HEADS * RET_DK
RET_V_W = RET_HEADS * RET_DV
RET_IN = 2 * RET_QK_W + 2 * RET_V_W
ATTN_HEADS = 16
ATTN_KV_HEADS = 4
ATTN_HEAD_DIM = D_MODEL // ATTN_HEADS
ATTN_GROUP = ATTN_HEADS // ATTN_KV_HEADS
WINDOW = 128
ATTN_BLOCK = 128
ATTN_Q_W = ATTN_HEADS * ATTN_HEAD_DIM
ATTN_KV_W = ATTN_KV_HEADS * ATTN_HEAD_DIM
ATTN_IN = 2 * ATTN_Q_W + 2 * ATTN_KV_W
ROPE_BASE = 10000.0
EPS = 1e-6
NEG_INF = -1e30

kernel_name = 'hybrid_retention_swa_prefix_dit'


def _rms_norm(x, gain):
    xf = x.astype(jnp.float32)
    y = xf * lax.rsqrt(jnp.mean(xf * xf, axis=-1, keepdims=True) + EPS)
    return (y * gain.astype(jnp.float32)).astype(x.dtype)


def _rope_axis(x, pos):
    n2 = x.shape[-1] // 2
    inv = ROPE_BASE ** (-jnp.arange(n2, dtype=jnp.float32) / n2)
    ang = pos.astype(jnp.float32)[:, None] * inv[None, :]
    cos, sin = jnp.cos(ang), jnp.sin(ang)
    xf = x.astype(jnp.float32)
    x1, x2 = xf[..., :n2], xf[..., n2:]
    return jnp.concatenate([x1 * cos - x2 * sin, x1 * sin + x2 * cos], axis=-1).astype(x.dtype)


def _rope_2d(x, row, col):
    half = x.shape[-1] // 2
    return jnp.concatenate([_rope_axis(x[..., :half], row), _rope_axis(x[..., half:], col)], axis=-1)


def _heads(t, n, d):
    b, l, _ = t.shape
    return t.reshape(b, l, n, d).transpose(0, 2, 1, 3)


def _retention_chunked(q, k, v, log_gamma, s0, inclusive):
    b, h, l, dk = q.shape
    dv = v.shape[-1]
    n = l // RET_CHUNK
    qc = q.reshape(b, h, n, RET_CHUNK, dk)
    kc = k.reshape(b, h, n, RET_CHUNK, dk)
    vc = v.reshape(b, h, n, RET_CHUNK, dv)
    idx = jnp.arange(RET_CHUNK, dtype=jnp.float32)
    diff = idx[:, None] - idx[None, :]
    allowed = (diff >= 0) if inclusive else (diff > 0)
    lg = log_gamma[:, None, None]
    decay = jnp.where(allowed[None], jnp.exp(lg * jnp.maximum(diff, 0.0)[None]), 0.0)
    scores = jnp.einsum('bhnid,bhnjd->bhnij', qc, kc) * decay[None, :, None]
    inner = jnp.einsum('bhnij,bhnjv->bhniv', scores, vc)
    q_w = jnp.exp(log_gamma[:, None] * (idx + 1.0)[None])[None, :, None, :, None]
    k_w = jnp.exp(log_gamma[:, None] * (RET_CHUNK - 1.0 - idx)[None])[None, :, None, :, None]
    chunk_decay = jnp.exp(log_gamma * RET_CHUNK)[None, :, None, None]
    xs = (jnp.moveaxis(qc * q_w, 2, 0), jnp.moveaxis(kc * k_w, 2, 0), jnp.moveaxis(vc, 2, 0))

    def step(state, blk):
        qb, kb, vb = blk
        out = jnp.einsum('bhid,bhdv->bhiv', qb, state)
        state = chunk_decay * state + jnp.einsum('bhjd,bhjv->bhdv', kb, vb)
        return state, out

    s_fin, cross = lax.scan(step, s0, xs)
    out = inner + jnp.moveaxis(cross, 0, 2)
    return out.reshape(b, h, l, dv), s_fin


def _retention_bidir(q, k, v, lg_f, lg_b, s0_f, s0_b):
    o_f, s_f = _retention_chunked(q, k, v, lg_f, s0_f, True)
    flip = lambda t: jnp.flip(t, axis=2)
    o_b, s_b = _retention_chunked(flip(q), flip(k), flip(v), lg_b, s0_b, False)
    return o_f + flip(o_b), s_f, s_b


def _retention_mixer(h_lat, h_ctx, w_in, logit_f, logit_b, norm_gain, w_out, row, col, with_ctx_out):
    lg_f = jax.nn.log_sigmoid(logit_f.astype(jnp.float32))
    lg_b = jax.nn.log_sigmoid(logit_b.astype(jnp.float32))

    def project(h):
        p = h @ w_in
        q = _heads(p[..., :RET_QK_W], RET_HEADS, RET_DK).astype(jnp.float32) * (RET_DK ** -0.5)
        k = _heads(p[..., RET_QK_W:2 * RET_QK_W], RET_HEADS, RET_DK).astype(jnp.float32)
        v = _heads(p[..., 2 * RET_QK_W:2 * RET_QK_W + RET_V_W], RET_HEADS, RET_DV).astype(jnp.float32)
        g = p[..., 2 * RET_QK_W + RET_V_W:]
        return q, k, v, g

    def finish(o, g):
        b, h, l, dv = o.shape
        on = o * lax.rsqrt(jnp.mean(o * o, axis=-1, keepdims=True) + EPS)
        on = on.transpose(0, 2, 1, 3).reshape(b, l, h * dv) * norm_gain.astype(jnp.float32)
        return (on.astype(g.dtype) * jax.nn.silu(g)) @ w_out

    q_c, k_c, v_c, g_c = project(h_ctx)
    q_l, k_l, v_l, g_l = project(h_lat)
    q_l = _rope_2d(q_l, row, col)
    k_l = _rope_2d(k_l, row, col)
    b = h_lat.shape[0]
    zeros = jnp.zeros((b, RET_HEADS, RET_DK, RET_DV), jnp.float32)
    o_c, s_f, s_b = _retention_bidir(q_c, k_c, v_c, lg_f, lg_b, zeros, zeros)
    o_l, _, _ = _retention_bidir(q_l, k_l, v_l, lg_f, lg_b, s_f, s_b)
    y_l = finish(o_l.astype(jnp.float32), g_l).astype(h_lat.dtype)
    y_c = finish(o_c, g_c).astype(h_ctx.dtype) if with_ctx_out else None
    return y_l, y_c


def _attn_out(o, g, w_out):
    b, _, _, l, _ = o.shape
    o = o.transpose(0, 3, 1, 2, 4).reshape(b, l, ATTN_Q_W)
    return (o * jax.nn.silu(g)) @ w_out


def _attention_mixer(h_lat, h_ctx, w_in, q_gain, k_gain, sink, w_out, row, col, with_ctx_out):
    qd, kvd = ATTN_Q_W, ATTN_KV_W
    scale = ATTN_HEAD_DIM ** -0.5
    sink_g = sink.astype(jnp.float32).reshape(ATTN_KV_HEADS, ATTN_GROUP)
    p_l = h_lat @ w_in
    q_l = _rms_norm(_heads(p_l[..., :qd], ATTN_HEADS, ATTN_HEAD_DIM), q_gain)
    k_l = _rms_norm(_heads(p_l[..., qd:qd + kvd], ATTN_KV_HEADS, ATTN_HEAD_DIM), k_gain)
    v_l = _heads(p_l[..., qd + kvd:qd + 2 * kvd], ATTN_KV_HEADS, ATTN_HEAD_DIM)
    g_l = p_l[..., qd + 2 * kvd:]
    q_l = _rope_2d(q_l, row, col)
    k_l = _rope_2d(k_l, row, col)
    b, _, l, _ = q_l.shape
    q_l = q_l.reshape(b, ATTN_KV_HEADS, ATTN_GROUP, l, ATTN_HEAD_DIM)
    p_c = h_ctx @ (w_in if with_ctx_out else w_in[:, qd:qd + 2 * kvd])
    off = qd if with_ctx_out else 0
    k_c = _rms_norm(_heads(p_c[..., off:off + kvd], ATTN_KV_HEADS, ATTN_HEAD_DIM), k_gain)
    v_c = _heads(p_c[..., off + kvd:off + 2 * kvd], ATTN_KV_HEADS, ATTN_HEAD_DIM)
    n_ctx = k_c.shape[2]

    nb = l // ATTN_BLOCK
    qb = q_l.reshape(b, ATTN_KV_HEADS, ATTN_GROUP, nb, ATTN_BLOCK, ATTN_HEAD_DIM)

    def band(t):
        tp = jnp.pad(t, ((0, 0), (0, 0), (ATTN_BLOCK, ATTN_BLOCK), (0, 0)))
        tp = tp.reshape(b, ATTN_KV_HEADS, nb + 2, ATTN_BLOCK, ATTN_HEAD_DIM)
        return jnp.concatenate([tp[:, :, :-2], tp[:, :, 1:-1], tp[:, :, 2:]], axis=3)

    kb, vb = band(k_l), band(v_l)
    s_win = jnp.einsum('bkgnid,bknjd->bkgnij', qb, kb).astype(jnp.float32) * scale
    s_ctx = jnp.einsum('bkgnid,bkcd->bkgnic', qb, k_c).astype(jnp.float32) * scale
    i_idx = jnp.arange(ATTN_BLOCK)[None, :, None]
    j_idx = jnp.arange(3 * ATTN_BLOCK)[None, None, :]
    n_idx = jnp.arange(nb)[:, None, None]
    q_pos = n_idx * ATTN_BLOCK + i_idx
    k_pos = n_idx * ATTN_BLOCK - ATTN_BLOCK + j_idx
    valid = (jnp.abs(q_pos - k_pos) <= WINDOW) & (k_pos >= 0) & (k_pos < l)
    s_win = jnp.where(valid, s_win, NEG_INF)
    sink_col = jnp.broadcast_to(sink_g[None, :, :, None, None, None], s_win.shape[:-1] + (1,))
    probs = jax.nn.softmax(jnp.concatenate([s_win, s_ctx, sink_col], axis=-1), axis=-1)
    p_win = probs[..., :3 * ATTN_BLOCK].astype(vb.dtype)
    p_ctx = probs[..., 3 * ATTN_BLOCK:3 * ATTN_BLOCK + n_ctx].astype(v_c.dtype)
    o_l = (jnp.einsum('bkgnij,bknjd->bkgnid', p_win, vb)
           + jnp.einsum('bkgnic,bkcd->bkgnid', p_ctx, v_c))
    o_l = o_l.reshape(b, ATTN_KV_HEADS, ATTN_GROUP, l, ATTN_HEAD_DIM)
    y_l = _attn_out(o_l, g_l, w_out)

    y_c = None
    if with_ctx_out:
        q_c = _rms_norm(_heads(p_c[..., :qd], ATTN_HEADS, ATTN_HEAD_DIM), q_gain)
        q_c = q_c.reshape(b, ATTN_KV_HEADS, ATTN_GROUP, n_ctx, ATTN_HEAD_DIM)
        s_c = jnp.einsum('bkgid,bkcd->bkgic', q_c, k_c).astype(jnp.float32) * scale
        sink_c = jnp.broadcast_to(sink_g[None, :, :, None, None], s_c.shape[:-1] + (1,))
        p_c_attn = jax.nn.softmax(jnp.concatenate([s_c, sink_c], axis=-1), axis=-1)[..., :n_ctx]
        o_c = jnp.einsum('bkgic,bkcd->bkgid', p_c_attn.astype(v_c.dtype), v_c)
        y_c = _attn_out(o_c, p_c[..., qd + 2 * kvd:], w_out)
    return y_l, y_c


def setup_inputs(seed: int = 0) -> dict:
    key = jax.random.key(seed)
    ks = jax.random.split(key, 20)
    f32 = jnp.float32

    def nrm(k, shape, s):
        return jax.random.normal(k, shape, f32) * s

    e = 5.0 + jnp.arange(RET_HEADS, dtype=f32)
    gamma_logit = jnp.log1p(-jnp.exp2(-e)) + e * math.log(2.0)
    return {
        'x': nrm(ks[0], (BATCH, SEQ, D_MODEL), 1.0),
        'c': nrm(ks[1], (BATCH, D_MODEL), 1.0),
        'ctx': nrm(ks[2], (BATCH, CTX_LEN, D_MODEL), 1.0),
        'c_ctx': nrm(ks[3], (D_MODEL,), 1.0),
        'norm_gain': 1.0 + nrm(ks[4], (DEPTH, D_MODEL), 0.1),
        'ada_w': nrm(ks[5], (DEPTH, D_MODEL, 3 * D_MODEL), 0.5 * D_MODEL ** -0.5),
        'ada_b': nrm(ks[6], (DEPTH, 3 * D_MODEL), 0.01),
        'ret_w_in': nrm(ks[7], (N_RET_LAYERS, D_MODEL, RET_IN), D_MODEL ** -0.5),
        'ret_decay_logit_fwd': gamma_logit[None] + nrm(ks[8], (N_RET_LAYERS, RET_HEADS), 0.1),
        'ret_decay_logit_bwd': gamma_logit[None] + nrm(ks[9], (N_RET_LAYERS, RET_HEADS), 0.1),
        'ret_norm_gain': 1.0 + nrm(ks[10], (N_RET_LAYERS, RET_V_W), 0.1),
        'ret_w_out': nrm(ks[11], (N_RET_LAYERS, RET_V_W, D_MODEL), RET_V_W ** -0.5),
        'attn_w_in': nrm(ks[12], (N_ATTN_LAYERS, D_MODEL, ATTN_IN), D_MODEL ** -0.5),
        'attn_q_gain': 1.0 + nrm(ks[13], (N_ATTN_LAYERS, ATTN_HEAD_DIM), 0.1),
        'attn_k_gain': 1.0 + nrm(ks[14], (N_ATTN_LAYERS, ATTN_HEAD_DIM), 0.1),
        'attn_sink': nrm(ks[15], (N_ATTN_LAYERS, ATTN_HEADS), 0.5),
        'attn_w_out': nrm(ks[16], (N_ATTN_LAYERS, ATTN_Q_W, D_MODEL), ATTN_Q_W ** -0.5),
    }


def reference(x, c, ctx, c_ctx, norm_gain, ada_w, ada_b, ret_w_in, ret_decay_logit_fwd,
              ret_decay_logit_bwd, ret_norm_gain, ret_w_out, attn_w_in, attn_q_gain,
              attn_k_gain, attn_sink, attn_w_out):
    l = x.shape[1]
    rows = l // GRID_W
    row = jnp.repeat(jnp.arange(rows, dtype=jnp.int32), GRID_W)
    col = jnp.tile(jnp.arange(GRID_W, dtype=jnp.int32), rows)
    c_act = jax.nn.silu(c)
    cc_act = jax.nn.silu(c_ctx)
    for i in range(DEPTH):
        j = i // N_MIXERS
        with_ctx_out = i < DEPTH - 1
        mod_l = c_act @ ada_w[i] + ada_b[i]
        mod_c = cc_act @ ada_w[i] + ada_b[i]
        sh_l, sc_l, gt_l = jnp.split(mod_l, 3, axis=-1)
        sh_c, sc_c, gt_c = jnp.split(mod_c, 3, axis=-1)
        h_l = _rms_norm(x, norm_gain[i]) * (1 + sc_l[:, None]) + sh_l[:, None]
        h_c = _rms_norm(ctx, norm_gain[i]) * (1 + sc_c) + sh_c
        if i % N_MIXERS == 0:
            y_l, y_c = _retention_mixer(h_l, h_c, ret_w_in[j], ret_decay_logit_fwd[j],
                                        ret_decay_logit_bwd[j], ret_norm_gain[j], ret_w_out[j],
                                        row, col, with_ctx_out)
        else:
            y_l, y_c = _attention_mixer(h_l, h_c, attn_w_in[j], attn_q_gain[j], attn_k_gain[j],
                                        attn_sink[j], attn_w_out[j], row, col, with_ctx_out)
        x = x + gt_l[:, None] * y_l
        if with_ctx_out:
            ctx = ctx + gt_c * y_c
    return x
```

```python
import math
from contextlib import ExitStack

import numpy as np
import concourse.bass as bass
import concourse.mybir as mybir
from concourse.bass_utils import run_bass_kernel_spmd

F32 = mybir.dt.float32
BF16 = mybir.dt.bfloat16
ALU = mybir.AluOpType
AF = mybir.ActivationFunctionType
AX = mybir.AxisListType

D = 2048
NT = 10
TOK = NT * 128
EPS = 1e-6
RG = [[0, 1], [2, 3], [4, 5], [6, 7]]
SBUF_BASE = 16512
SBUF_END = 229376


class Buf:
    __slots__ = ("w", "r", "name")

    def __init__(self, name=""):
        self.w = None
        self.r = {}
        self.name = name


class Sched:
    ENG = ("sync", "act", "pool", "dve", "pe")

    def __init__(self):
        self.streams = {e: [] for e in self.ENG}
        self.cnt = {e: 0 for e in self.ENG}
        self.waited = {e: {} for e in self.ENG}
        self.chans = {}

    def _val(self, key, val):
        if key[0] == "D":
            c = self.chans[key[1]]
            return c[0] * c[1]
        return val

    def op(self, eng, fn, reads=(), writes=(), chan=None, chan_inc=16, inc=True):
        need = {}

        def add(key, val):
            if not (chan is not None and key == ("D", chan)):
                val = self._val(key, val)
            if key == ("E", "pe") and eng == "pe":
                return
            if need.get(key, 0) < val:
                need[key] = val

        for b in reads:
            if b.w is not None:
                add(*b.w)
        for b in writes:
            if b.w is not None:
                add(*b.w)
            for k, v in b.r.items():
                add(k, v)
        wd = self.waited[eng]
        waits = []
        for k, v in need.items():
            if wd.get(k, 0) < v:
                wd[k] = v
                waits.append((k, v))
        if chan is not None:
            c = self.chans.setdefault(chan, [0, chan_inc])
            c[0] += 1
            ev = (("D", chan), c[0] * c[1])
            rec = ev
        elif inc:
            self.cnt[eng] += 1
            ev = (("E", eng), self.cnt[eng])
            rec = ev
        else:
            ev = (("E", eng), self.cnt[eng] + 1)
            rec = None
        self.streams[eng].append((waits, fn, rec))
        for b in reads:
            if b.r.get(ev[0], 0) < ev[1]:
                b.r[ev[0]] = ev[1]
        for b in writes:
            b.w = ev
            b.r = {}
        return ev

    def dve(self, fn, reads=(), writes=()):
        return self.op("dve", fn, reads, writes)

    def act(self, fn, reads=(), writes=()):
        return self.op("act", fn, reads, writes)

    def pool(self, fn, reads=(), writes=()):
        return self.op("pool", fn, reads, writes)

    def pe(self, fn, reads=(), writes=(), inc=True):
        return self.op("pe", fn, reads, writes, inc=inc)

    def dma(self, q, chan, out, in_, reads=(), writes=(), **kw):
        return self.op(q, lambda e: e.dma_start(out=out, in_=in_, **kw), reads, writes, chan=chan)


def f_mm(out, lhsT, rhs, st, sp):
    return lambda e: e.matmul(out, lhsT, rhs, start=st, stop=sp)


def f_tr(out, in_, ident):
    return lambda e: e.transpose(out, in_, ident)


def f_act(out, in_, func, **kw):
    return lambda e: e.activation(out=out, in_=in_, func=func, **kw)


def f_acopy(out, in_):
    return lambda e: e.copy(out=out, in_=in_)


def f_amul(out, in_, m):
    return lambda e: e.mul(out=out, in_=in_, mul=m)


def f_tt(out, a, b, op):
    return lambda e: e.tensor_tensor(out=out, in0=a, in1=b, op=op)


def f_ts(out, a, s1, s2, op0, op1=None):
    if op1 is None:
        return lambda e: e.tensor_scalar(out=out, in0=a, scalar1=s1, scalar2=None, op0=op0)
    return lambda e: e.tensor_scalar(out=out, in0=a, scalar1=s1, scalar2=s2, op0=op0, op1=op1)


def f_stt(out, a, s, b, op0, op1):
    return lambda e: e.scalar_tensor_tensor(out=out, in0=a, scalar=s, in1=b, op0=op0, op1=op1)


def f_copy(out, in_):
    return lambda e: e.tensor_copy(out=out, in_=in_)


def f_memset(ap, v):
    return lambda e: e.memset(ap, v)


def f_rsum(out, in_):
    return lambda e: e.reduce_sum(out=out, in_=in_, axis=AX.X)


def f_recip(out, in_):
    return lambda e: e.reciprocal(out=out, in_=in_)


DT_SIZE = {F32: 4, BF16: 2}


class Arena:
    def __init__(self, prog, name, base, size):
        self.p = prog
        self.name = name
        self.base = base
        self.size = size
        self.off = 0
        self.gen = 0
        self.bufs = []
        self.pending = {}

    def reset(self):
        pend = dict(self.pending)
        for b in self.bufs:
            if b.w is not None:
                k, v = b.w
                v = self.p.S._val(k, v)
                if pend.get(k, 0) < v:
                    pend[k] = v
            for k, v in b.r.items():
                v = self.p.S._val(k, v)
                if pend.get(k, 0) < v:
                    pend[k] = v
        self.pending = pend
        self.bufs = []
        self.off = 0
        self.gen += 1

    def alloc(self, name, shape, dtype):
        n = 1
        for s in shape[1:]:
            n *= s
        nbytes = (n * DT_SIZE[dtype] + 31) // 32 * 32
        assert self.off + nbytes <= self.size, (self.name, name, self.off, nbytes, self.size)
        h = self.p.nc.alloc_sbuf_tensor_at(f"{self.name}{self.gen}_{name}", list(shape), dtype,
                                           offset=self.base + self.off)
        self.off += nbytes
        b = Buf(name)
        b.r = dict(self.pending)
        self.bufs.append(b)
        return h, b

    def allocn(self, name, n, shape, dtype):
        hs, bs = [], []
        for i in range(n):
            h, b = self.alloc(f"{name}{i}", shape, dtype)
            hs.append(h)
            bs.append(b)
        return hs, bs


class Prog:
    def __init__(self, layers=(0, 1, 2, 3), dbg=False):
        self.layers = tuple(layers)
        self.dbg = dbg
        self.nc = bass.Bass("TRN2", target_bir_lowering=False)
        self.S = Sched()
        self.dram_bufs = {}
        self._sb_off = SBUF_BASE
        self.build()

    def din(self, name, shape, dt=F32):
        return self.nc.dram_tensor(name, list(shape), dt, kind="ExternalInput")

    def dscr(self, name, shape, dt):
        return self.nc.dram_tensor(name, list(shape), dt)

    def sb(self, name, shape, dt):
        n = 1
        for s in shape[1:]:
            n *= s
        nbytes = (n * DT_SIZE[dt] + 31) // 32 * 32
        h = self.nc.alloc_sbuf_tensor_at(name, list(shape), dt, offset=self._sb_off)
        self._sb_off += nbytes
        return h

    def db(self, *key):
        b = self.dram_bufs.get(key)
        if b is None:
            b = Buf(str(key))
            self.dram_bufs[key] = b
        return b

    def build(self):
        nc, S = self.nc, self.S
        L = self.layers
        ret_js = sorted({i // 2 for i in L if i % 2 == 0})
        att_js = sorted({i // 2 for i in L if i % 2 == 1})
        self.x0 = self.din("x0", [NT, 128, D])
        self.cvec = self.din("cvec", [2, D])
        self.ada_t = {i: self.din(f"ada_t{i}", [6, 128, 8192]) for i in L}
        self.ada_bh = self.din("ada_bh", [4, 3072])
        self.ng_cols = self.din("ng_cols", [4, 128, 16])
        self.ret_win = {j: self.din(f"ret_win{j}", [24, 128, 8192]) for j in ret_js}
        self.ret_wout = {j: self.din(f"ret_wout{j}", [8, 128, 8192]) for j in ret_js}
        self.ret_logit = self.din("ret_logit", [2, 16])
        self.ret_gain = self.din("ret_gain", [2, 4096])
        self.att_win = {j: self.din(f"att_win{j}", [10, 128, 8192]) for j in att_js}
        self.att_wout = {j: self.din(f"att_wout{j}", [4, 128, 8192]) for j in att_js}
        self.att_qg = self.din("att_qg", [2, 128])
        self.att_kg = self.din("att_kg", [2, 128])
        self.att_sink = self.din("att_sink", [2, 16])
        self.rope_r = self.din("rope_r", [NT, 128, 512])
        self.rope_a = self.din("rope_a", [NT, 128, 256])
        self.rmask_d = self.din("rmask", [128, 256])
        self.amask_d = self.din("amask", [128, 384])
        self.ecol_d = self.din("ecol", [128, 8])
        self.erow_d = self.din("erow", [128, 256])
        self.sel_d = self.din("sel", [128, 2])
        self.ident_d = self.din("ident", [128, 128])
        self.y = nc.dram_tensor("y", [8, 128, D], F32, kind="ExternalOutput")
        if self.dbg:
            self.yctx = nc.dram_tensor("yctx", [2, 128, D], F32, kind="ExternalOutput")
        self.mod_half = self.dscr("mod_half", [8, 3072], F32)
        self.mod_full = self.dscr("mod_full", [16, 3072], F32)
        self.qT_s = self.dscr("qT_s", [NT, 8, 128, 256], BF16)
        self.kT_s = self.dscr("kT_s", [2, NT, 8, 128, 256], BF16)
        self.kk_s = self.dscr("kk_s", [2, NT, 128, 8, 256], BF16)
        self.v_s = self.dscr("v_s", [NT, 128, 8, 512], BF16)
        self.sg_s = self.dscr("sg_s", [NT, 128, 8, 512], BF16)
        self.o1_s = self.dscr("o1_s", [NT, 128, 8, 512], F32)
        self.uT_s = self.dscr("uT_s", [32, 128, TOK], BF16)
        self.st_out = self.dscr("st_out", [8, 128, 1024], F32)
        self.st_in = self.dscr("st_in", [8, 256, 1024], F32)
        self.aqT_s = self.dscr("aqT_s", [NT, 128, 16, 128], BF16)
        self.akT_s = self.dscr("akT_s", [NT, 128, 4, 128], BF16)
        self.av_s = self.dscr("av_s", [NT, 128, 4, 128], BF16)
        self.asg_s = self.dscr("asg_s", [NT, 128, D], BF16)
        self.halo_out = self.dscr("halo_out", [128, 1024], BF16)
        self.halo_in = self.dscr("halo_in", [256, 1024], BF16)
        self.x_sb = self.sb("x_sb", [128, NT, D], F32)
        self.xB = [Buf(f"x{t}") for t in range(NT)]
        self.wbuf = self.sb("wbuf", [128, 2, 8192], BF16)
        self.wB = [Buf("w0"), Buf("w1")]
        self.wslot = 0
        self.G = self.sb("G", [128, 2, D], F32)
        self.GB = Buf("G")
        self.ident_f = self.sb("ident_f", [128, 128], F32)
        self.ident_b = self.sb("ident_b", [128, 128], BF16)
        self.identB = Buf("ident")
        self.sel = self.sb("sel", [128, 2], F32)
        self.ecol = self.sb("ecol", [128, 8], F32)
        self.erow = self.sb("erow", [128, 256], F32)
        self.rmask = self.sb("rmask", [128, 2, 128], F32)
        self.amask = self.sb("amask", [128, 3, 128], F32)
        self.constB = Buf("const")
        self.AB = self.sb("AB", [128, 4, 2, 2, 16], F32)
        self.ABB = Buf("AB")
        self.cT_f = self.sb("cT_f", [128, 2, 16], F32)
        self.cT_b = self.sb("cT_b", [128, 16, 2], BF16)
        self.cTB = Buf("cT")
        self.trow = self.sb("trow", [128, 2, 8, 128], F32)
        self.rcol = self.sb("rcol", [128, 2, 8, 4], F32)
        self.lrep = self.sb("lrep", [128, 16], F32)
        self.rtabB = Buf("rtab")
        self.es = self.sb("es", [128, 16], F32)
        self.qkg = self.sb("qkg", [128, 2, 128], F32)
        self.atabB = Buf("atab")
        self.neghalf = self.sb("neghalf", [128, 1], F32)
        self.small = self.sb("small", [128, 64], F32)
        used = self._sb_off
        rem = SBUF_END - used
        usz = 40960
        self.U = Arena(self, "U", used, usz)
        self.M = Arena(self, "M", used + usz, rem - usz)
        self.pf = [nc.alloc_psum_tensor(f"pf{i}", [128, 512], F32) for i in range(6)]
        self.pfB = [Buf(f"pf{i}") for i in range(6)]
        self.pb = [nc.alloc_psum_tensor(f"pb{i}", [128, 1024], BF16) for i in range(2)]
        self.pbB = [Buf(f"pb{i}") for i in range(2)]
        self.pf_i = 0
        self.pb_i = 0

        self.prologue()
        for i in L:
            self.layer(i)
        self.epilogue()
        self.emit()

    def next_pf(self, lo=0, hi=6):
        self.pf_i += 1
        i = lo + self.pf_i % (hi - lo)
        return self.pf[i], self.pfB[i]

    def next_pb(self):
        self.pb_i += 1
        i = self.pb_i % 2
        return self.pb[i], self.pbB[i]

    def load_w(self, src_ap):
        s = self.wslot
        self.wslot ^= 1
        self.S.dma("pool", f"w{s}", self.wbuf[:, s, :], src_ap, writes=[self.wB[s]])
        return s

    def prologue(self):
        S, nc = self.S, self.nc
        cb = self.constB
        S.dma("sync", "const", self.ident_f[:, :], self.ident_d[:, :], writes=[self.identB])
        S.dma("sync", "const", self.sel[:, :], self.sel_d[:, :], writes=[cb])
        S.dma("sync", "const", self.ecol[:, :], self.ecol_d[:, :], writes=[cb])
        S.dma("sync", "const", self.erow[:, :], self.erow_d[:, :], writes=[cb])
        S.dma("sync", "const", self.rmask[:, :, :], self.rmask_d.ap().rearrange("p (a b) -> p a b", a=2), writes=[cb])
        S.dma("sync", "const", self.amask[:, :, :], self.amask_d.ap().rearrange("p (a b) -> p a b", a=3), writes=[cb])
        S.act(f_acopy(self.ident_b[:, :], self.ident_f[:, :]), reads=[self.identB], writes=[self.identB])
        S.dve(f_memset(self.neghalf[:, :], -0.5), writes=[cb])
        for t in range(NT):
            S.dma("sync", "xin", self.x_sb[:, t, :], self.x0[t], writes=[self.xB[t]])
        for r in range(2):
            S.dma("sync", "const", self.cT_f[:, r, :], self.cvec[r, :].rearrange("(k p) -> p k", p=128),
                  writes=[self.cTB], allow_slow_non_contiguous=True)
        S.act(f_act(self.cT_b[:, :, :], self.cT_f[:, :, :].rearrange("p r k -> p k r"), AF.Silu),
              reads=[self.cTB], writes=[self.cTB])
        self.M.reset()
        bias_t, bias_b = self.M.allocn("adab", 2, [2, 512], F32)
        mrow_t, mrow_b = self.M.allocn("mrow", 2, [2, 512], F32)
        it = 0
        for i in self.layers:
            for g in range(6):
                s = self.load_w(self.ada_t[i][g])
                sl = it % 2
                it += 1
                S.dma("sync", f"adab{sl}", bias_t[sl][:, :],
                      self.ada_bh[i, g * 512:(g + 1) * 512].partition_broadcast(2), writes=[bias_b[sl]])
                pf, pfb = self.next_pf()
                for k in range(16):
                    S.pe(f_mm(pf[0:2, :], self.cT_b[:, k, :], self.wbuf[:, s, k * 512:(k + 1) * 512], k == 0, k == 15),
                         reads=[self.cTB, self.wB[s]], writes=[pfb], inc=(k == 15))
                S.dve(f_tt(mrow_t[sl][:, :], pf[0:2, :], bias_t[sl][:, :], ALU.add),
                      reads=[pfb, bias_b[sl]], writes=[mrow_b[sl]])
                S.dma("sync", f"modst{sl}", self.mod_half[2 * i:2 * i + 2, g * 512:(g + 1) * 512], mrow_t[sl][:, :],
                      reads=[mrow_b[sl]], writes=[self.db("mod_half")])
        S.op("pool", lambda e: e.collective_compute("AllGather", ALU.bypass, replica_groups=RG,
                                                     ins=[self.mod_half.ap()], outs=[self.mod_full.ap()]),
             reads=[self.db("mod_half")], writes=[self.db("mod_full")], chan="cc_mod", chan_inc=1)
        mf = self.mod_full
        ngt, ngb = self.M.alloc("ng", [128, 4, 16], F32)
        S.dma("sync", "const", ngt[:, :, :], self.ng_cols.ap().rearrange("l p k -> p l k"), writes=[ngb])
        sct, scb = self.M.alloc("sc", [128, 4, 2, 16], F32)
        for i in self.layers:
            for r in range(2):
                row = 2 * i + r
                kw = dict(allow_slow_non_contiguous=True)
                S.dma("sync", "const", self.AB[:, i, r, 1, :],
                      mf[row, 0:2048].rearrange("(k p) -> p k", p=128),
                      reads=[self.db("mod_full")], writes=[self.ABB], **kw)
                S.dma("sync", "const", sct[:, i, r, 0:8],
                      mf[row, 2048:3072].rearrange("(k p) -> p k", p=128),
                      reads=[self.db("mod_full")], writes=[scb], **kw)
                S.dma("sync", "const", sct[:, i, r, 8:16],
                      mf[8 + row, 0:1024].rearrange("(k p) -> p k", p=128),
                      reads=[self.db("mod_full")], writes=[scb], **kw)
            S.dve(f_ts(sct[:, i, :, :], sct[:, i, :, :], 1.0, math.sqrt(D), ALU.add, ALU.mult),
                  reads=[scb], writes=[scb])
            for r in range(2):
                S.dve(f_tt(self.AB[:, i, r, 0, :], sct[:, i, r, :], ngt[:, i, :], ALU.mult),
                      reads=[scb, ngb, self.ABB], writes=[self.ABB])

    def load_gate(self, i):
        S = self.S
        for r in range(2):
            S.dma("sync", "gate", self.G[:, r, :], self.mod_full[8 + 2 * i + r, 1024:3072].partition_broadcast(128),
                  reads=[self.db("mod_full")], writes=[self.GB])

    def norm_phase(self, i):
        S = self.S
        self.U.reset()
        self.M.reset()
        hT, hTB0 = self.U.alloc("hT", [128, 16, TOK], BF16)
        self.hT = hT
        self.hTB = [Buf(f"hT{t}") for t in range(NT)]
        for b in self.hTB:
            b.r = dict(hTB0.r)
        self.U.bufs.extend(self.hTB)
        xn, xnB = self.M.allocn("xn", 2, [128, D], F32)
        tmp, tmpB = self.M.allocn("ntmp", 2, [128, 4, 128], F32)
        ss, ssB = self.M.allocn("nss", 2, [128, 4], F32)
        ti = 0
        for t in range(NT):
            r = 1 if t < 2 else 0
            sl = t % 2
            S.act(f_act(xn[sl][:, :], self.x_sb[:, t, :], AF.Square, accum_out=ss[sl][:, 0:1]),
                  reads=[self.xB[t]], writes=[xnB[sl], ssB[sl]])
            S.dve(f_ts(ss[sl][:, 1:2], ss[sl][:, 0:1], float(D * EPS), None, ALU.add), reads=[ssB[sl]], writes=[ssB[sl]])
            S.pool(f_tt(ss[sl][:, 2:3], ss[sl][:, 1:2], self.neghalf[:, :], ALU.pow),
                   reads=[ssB[sl], self.constB], writes=[ssB[sl]])
            S.dve(f_ts(xn[sl][:, :], self.x_sb[:, t, :], ss[sl][:, 2:3], None, ALU.mult),
                  reads=[self.xB[t], ssB[sl]], writes=[xnB[sl]])
            for q4 in range(4):
                pf, pfb = self.next_pf()
                for kk in range(4):
                    k = q4 * 4 + kk
                    S.pe(f_tr(pf[:, kk * 128:(kk + 1) * 128], xn[sl][:, k * 128:(k + 1) * 128], self.ident_f[:, :]),
                         reads=[xnB[sl], self.identB], writes=[pfb], inc=(kk == 3))
                s2 = ti % 2
                ti += 1
                a_bc = self.AB[:, i, r, 0, q4 * 4:q4 * 4 + 4].unsqueeze(2).broadcast_to([128, 4, 128])
                b_bc = self.AB[:, i, r, 1, q4 * 4:q4 * 4 + 4].unsqueeze(2).broadcast_to([128, 4, 128])
                S.dve(f_tt(tmp[s2][:, :, :], pf[:, :].rearrange("p (a b) -> p a b", a=4), a_bc, ALU.mult),
                      reads=[pfb, self.ABB], writes=[tmpB[s2]])
                S.pool(f_tt(hT[:, q4 * 4:q4 * 4 + 4, t * 128:(t + 1) * 128], tmp[s2][:, :, :], b_bc, ALU.add),
                       reads=[tmpB[s2], self.ABB], writes=[self.hTB[t]])

    def proj_phase(self, groups):
        S = self.S
        slots = [None] * len(groups)
        if groups:
            slots[0] = self.load_w(groups[0][0])
        pend = []

        def advance():
            nonlocal pend
            newp = []
            for lst in pend:
                lst.pop(0)()
                if lst:
                    newp.append(lst)
            pend = newp

        for gi, (w_ap, tiles, epi) in enumerate(groups):
            if gi + 1 < len(groups):
                slots[gi + 1] = self.load_w(groups[gi + 1][0])
            s = slots[gi]
            for t in tiles:
                pf, pfb = self.next_pf(0, 4)
                for k in range(16):
                    S.pe(f_mm(pf[:, :], self.hT[:, k, t * 128:(t + 1) * 128], self.wbuf[:, s, k * 512:(k + 1) * 512],
                              k == 0, k == 15),
                         reads=[self.hTB[t], self.wB[s]], writes=[pfb], inc=(k == 15))
                stages = epi(t, pf, pfb)
                advance()
                if stages:
                    pend.append(list(stages))
        while pend:
            advance()

    def outproj_phase(self, w_dram, nk, tiles, final_store=False):
        S = self.S
        ncol = 8192 // nk
        ng = D // ncol
        nst = 1 if nk == 16 else 2
        per = (len(tiles) + nst - 1) // nst
        self.U.reset()
        self.M.reset()
        uT, uTB = self.U.alloc("uT", [128, nk, 20480 // nk], BF16)
        tmp, tmpB = self.M.allocn("otmp", 2, [128, 512], F32)
        ti = 0
        for st in range(nst):
            tl = tiles[st * per:(st + 1) * per]
            if not tl:
                continue
            t0, ntok = tl[0], len(tl) * 128
            S.dma("sync", "uTld", uT[:, :, 0:ntok], self.uT_s[0:nk, :, t0 * 128:t0 * 128 + ntok].rearrange("c f j -> f c j"),
                  reads=[self.db("uT_s")], writes=[uTB])
            slots = [None] * ng
            slots[0] = self.load_w(w_dram[0])
            for g in range(ng):
                if g + 1 < ng:
                    slots[g + 1] = self.load_w(w_dram[g + 1])
                s = slots[g]
                for li, t in enumerate(tl):
                    pf, pfb = self.next_pf(0, 4)
                    for k in range(nk):
                        S.pe(f_mm(pf[:, 0:ncol], uT[:, k, li * 128:(li + 1) * 128],
                                  self.wbuf[:, s, k * ncol:(k + 1) * ncol], k == 0, k == nk - 1),
                             reads=[uTB, self.wB[s]], writes=[pfb], inc=(k == nk - 1))
                    r = 1 if t < 2 else 0
                    s2 = ti % 2
                    ti += 1
                    cs = slice(g * ncol, (g + 1) * ncol)
                    S.dve(f_tt(tmp[s2][:, 0:ncol], pf[:, 0:ncol], self.G[:, r, cs], ALU.mult),
                          reads=[pfb, self.GB], writes=[tmpB[s2]])
                    S.pool(f_tt(self.x_sb[:, t, cs], self.x_sb[:, t, cs], tmp[s2][:, 0:ncol], ALU.add),
                           reads=[tmpB[s2], self.xB[t]], writes=[self.xB[t]])

    def layer(self, i):
        self.load_gate(i)
        self.norm_phase(i)
        if i % 2 == 0:
            self.ret_layer(i)
        else:
            self.att_layer(i)

    def ret_tables(self, j):
        S = self.S
        tb = self.rtabB
        S.dma("sync", "const", self.lrep[:, :], self.ret_logit[j, :].partition_broadcast(128), writes=[tb])
        S.act(f_act(self.lrep[:, :], self.lrep[:, :], AF.Exp, scale=-1.0), reads=[tb], writes=[tb])
        S.act(f_act(self.lrep[:, :], self.lrep[:, :], AF.Ln, bias=1.0), reads=[tb], writes=[tb])
        for s in range(2):
            for h in range(8):
                lc = self.lrep[:, s * 8 + h:s * 8 + h + 1]
                S.act(f_act(self.trow[:, s, h, :], self.erow[:, s * 128:(s + 1) * 128], AF.Exp, scale=lc),
                      reads=[tb, self.constB], writes=[tb])
                S.act(f_act(self.rcol[:, s, h, :], self.ecol[:, s * 4:(s + 1) * 4], AF.Exp, scale=lc),
                      reads=[tb, self.constB], writes=[tb])

    def ret_layer(self, i):
        S = self.S
        j = i // 2
        self.ret_tables(j)
        self.M.reset()
        M = self.M
        tab, tabB = M.allocn("tab", 2, [128, 512], F32)
        xs, xsB = M.allocn("xs", 2, [128, 512], F32)
        bb, bbB = M.allocn("bb", 2, [128, 512], F32)
        qr, qrB = M.allocn("qr", 2, [128, 512], BF16)
        tpc2, tpcB2 = M.allocn("tpc", 2, [128, 4, 128], BF16)
        kT12, kT12B = M.alloc("kT12", [128, 2, 4, 128], BF16)
        k12, k12B = M.alloc("k12", [128, 2, 512], BF16)
        vg, vgB = M.allocn("vg", 2, [128, 512], BF16)
        gn, gnB = M.alloc("gn", [128, 512], F32)
        cnt = {"e": 0}

        def epi_qk(kind, gq):
            def epi(t, pf, pfb):
                sl = cnt["e"] % 2
                cnt["e"] += 1
                tpc, tpcB = tpc2[sl], tpcB2[sl]
                S.dma("sync", f"tab{sl}", tab[sl][:, :], self.rope_r[t], writes=[tabB[sl]])
                S.act(f_act(xs[sl][:, :], pf[:, :], AF.Copy, scale=(1.0 / 16.0 if kind == "q" else 1.0)),
                      reads=[pfb], writes=[xsB[sl]])

                def stage2():
                    xv = xs[sl][:, :].rearrange("p (h a b c) -> p h a b c", h=2, a=2, b=2)
                    bv = bb[sl][:, :].rearrange("p (h a b c) -> p h a b c", h=2, a=2, b=2)
                    cos_bc = tab[sl][:, 0:256].unsqueeze(1).broadcast_to([128, 2, 256])
                    sinv = tab[sl][:, 256:512].rearrange("p (a b c) -> p a b c", a=2, b=2)
                    for x12 in range(2):
                        S.dve(f_tt(bv[:, :, :, x12, :], xv[:, :, :, 1 - x12, :],
                                   sinv[:, :, x12, :].unsqueeze(1).broadcast_to([128, 2, 2, 64]), ALU.mult),
                              reads=[xsB[sl], tabB[sl]], writes=[bbB[sl]])
                    S.dve(f_tt(xs[sl][:, :].rearrange("p (a b) -> p a b", a=2),
                               xs[sl][:, :].rearrange("p (a b) -> p a b", a=2), cos_bc, ALU.mult),
                          reads=[xsB[sl], tabB[sl]], writes=[xsB[sl]])
                    (S.pool if kind == "q" else S.dve)(f_tt(qr[sl][:, :], xs[sl][:, :], bb[sl][:, :], ALU.add),
                                                       reads=[xsB[sl], bbB[sl]], writes=[qrB[sl]])

                def stage3():
                    pb, pbb = self.next_pb()
                    for c in range(4):
                        S.pe(f_tr(pb[:, c * 128:(c + 1) * 128], qr[sl][:, c * 128:(c + 1) * 128], self.ident_b[:, :]),
                             reads=[qrB[sl], self.identB], writes=[pbb], inc=(c == 3))
                    S.act(f_acopy(tpc[:, :, :], pb[:, 0:512].rearrange("p (a b) -> p a b", a=4)), reads=[pbb], writes=[tpcB])
                    if kind == "q":
                        S.dma("act", f"st_q{sl}", self.qT_s[t, 2 * gq:2 * gq + 2].rearrange("h d x -> d h x"),
                              tpc[:, :, :].rearrange("p (h c) j -> p h (c j)", h=2),
                              reads=[tpcB], writes=[self.db("qT", t)])
                        return
                    for s in range(2):
                        S.pool(f_tt(kT12[:, s, :, :].rearrange("p (h c) j -> p h c j", h=2),
                                    tpc[:, :, :].rearrange("p (h c) j -> p h c j", h=2),
                                    self.trow[:, s, 2 * gq:2 * gq + 2, :].unsqueeze(2).broadcast_to([128, 2, 2, 128]), ALU.mult),
                               reads=[tpcB, self.rtabB], writes=[kT12B])
                        for hh in range(2):
                            h = 2 * gq + hh
                            S.act(f_amul(k12[:, s, hh * 256:(hh + 1) * 256], qr[sl][:, hh * 256:(hh + 1) * 256],
                                         self.rcol[:, s, h, 1:2]),
                                  reads=[qrB[sl], self.rtabB], writes=[k12B])
                        S.dma("sync", "st_kT", self.kT_s[s, t, 2 * gq:2 * gq + 2].rearrange("h d x -> d h x"),
                              kT12[:, s, :, :].rearrange("p (h c) j -> p h (c j)", h=2),
                              reads=[kT12B], writes=[self.db("kT", t)])
                        S.dma("act", "st_kk", self.kk_s[s, t, :, 2 * gq:2 * gq + 2, :],
                              k12[:, s, :].rearrange("p (h x) -> p h x", h=2),
                              reads=[k12B], writes=[self.db("kk", t)])
                return [stage2, stage3]
            return epi

        def epi_v(h):
            def epi(t, pf, pfb):
                sl = cnt["e"] % 2
                cnt["e"] += 1
                S.act(f_acopy(vg[sl][:, :], pf[:, :]), reads=[pfb], writes=[vgB[sl]])
                S.dma("act", f"st_v{sl}", self.v_s[t, :, h, :], vg[sl][:, :], reads=[vgB[sl]], writes=[self.db("v", t)])
            return epi

        def epi_g(h):
            def epi(t, pf, pfb):
                sl = cnt["e"] % 2
                cnt["e"] += 1
                if t == 0:
                    S.dma("sync", "gn", gn[:, :], self.ret_gain[j, h * 512:(h + 1) * 512].partition_broadcast(128),
                          writes=[gnB])
                S.act(f_act(xs[sl][:, :], pf[:, :], AF.Silu), reads=[pfb], writes=[xsB[sl]])
                S.dve(f_tt(vg[sl][:, :], xs[sl][:, :], gn[:, :], ALU.mult), reads=[xsB[sl], gnB], writes=[vgB[sl]])
                S.dma("sync", f"st_v{sl}", self.sg_s[t, :, h, :], vg[sl][:, :], reads=[vgB[sl]], writes=[self.db("sg", t)])
            return epi

        w = self.ret_win[j]
        alltiles = list(range(NT))
        groups = []
        for gq in range(4):
            groups.append((w[gq], alltiles, epi_qk("q", gq)))
        for gq in range(4):
            groups.append((w[4 + gq], alltiles, epi_qk("k", gq)))
        for h in range(8):
            groups.append((w[8 + h], alltiles, epi_v(h)))
        for h in range(8):
            groups.append((w[16 + h], alltiles, epi_g(h)))
        self.proj_phase(groups)
        self.ret_mixer(j)
        self.outproj_phase(self.ret_wout[j], 32, list(range(NT)))

    def ret_mixer(self, j):
        S = self.S
        self.U.reset()
        self.M.reset()
        U, M = self.U, self.M
        Sf, SfB = U.allocn("Sf", 2, [128, 2, 512], F32)
        Sb, SbB = U.allocn("Sb", 2, [128, 2, 512], BF16)
        Gs, GsB = U.allocn("Gs", 2, [128, 1024], F32)
        NS = 5
        qT, qTB = U.allocn("qT", NS, [128, 2, 128], BF16)
        kT, kTB = U.allocn("kT", NS, [128, 2, 128], BF16)
        kk, kkB = U.allocn("kk", NS, [128, 256], BF16)
        vv, vvB = U.allocn("vv", NS, [128, 512], BF16)
        Pm, PmB = U.allocn("Pm", 2, [128, 128], BF16)
        o1o, o1oB = U.allocn("o1o", 2, [128, 512], F32)
        o1i, o1iB = M.allocn("o1i", NS, [128, 512], F32)
        sgi, sgiB = M.allocn("sgi", NS, [128, 512], BF16)
        of, ofB = M.allocn("of", 3, [128, 512], F32)
        junk, junkB = U.alloc("junk", [128, 512], BF16)
        uu, uuB = M.allocn("uu", 2, [128, 512], BF16)
        uTs, uTsB = U.allocn("uTs", 2, [128, 4, 128], BF16)
        fs, fsB = M.allocn("fs", 3, [128, 4], F32)
        st = {"ld": 0, "it": 0, "ita": 0}

        def issue_loads(s, h, t, final=False):
            sl = st["ld"] % NS
            st["ld"] += 1
            if final:
                S.dma("sync", f"rq{sl}", o1i[sl][:, :], self.o1_s[t, :, h, :], reads=[self.db("o1", t, h)], writes=[o1iB[sl]])
                S.dma("sync", f"rq{sl}", sgi[sl][:, :], self.sg_s[t, :, h, :], reads=[self.db("sg", t)], writes=[sgiB[sl]])
            S.dma("sync", f"rq{sl}", qT[sl][:, :, :], self.qT_s[t, h].rearrange("d (c j) -> d c j", c=2),
                  reads=[self.db("qT", t)], writes=[qTB[sl]])
            S.dma("sync", f"rq{sl}", kT[sl][:, :, :], self.kT_s[s, t, h].rearrange("d (c j) -> d c j", c=2),
                  reads=[self.db("kT", t)], writes=[kTB[sl]])
            S.dma("sync", f"rq{sl}", kk[sl][:, :], self.kk_s[s, t, :, h, :], reads=[self.db("kk", t)], writes=[kkB[sl]])
            S.dma("sync", f"rq{sl}", vv[sl][:, :], self.v_s[t, :, h, :], reads=[self.db("v", t)], writes=[vvB[sl]])
            return sl

        def step_a(s, h, t, sl):
            it = st["ita"]
            st["ita"] += 1
            p2 = it % 2
            sc, scB = self.next_pf(0, 2)
            for c in range(2):
                S.pe(f_mm(sc[:, 0:128], kT[sl][:, c, :], qT[sl][:, c, :], c == 0, c == 1),
                     reads=[kTB[sl], qTB[sl]], writes=[scB], inc=(c == 1))
            S.dve(f_tt(Pm[p2][:, :], sc[:, 0:128], self.rmask[:, s, :], ALU.mult),
                  reads=[scB, self.constB], writes=[PmB[p2]])

        def step(s, h, t, sl, hp, final):
            it = st["it"]
            st["it"] += 1
            p2 = it % 2
            for c in range(2):
                S.pe(f_mm(self.pf[4 + c][:, :], kk[sl][:, c * 128:(c + 1) * 128], vv[sl][:, :], True, True),
                     reads=[kkB[sl], vvB[sl]], writes=[self.pfB[4 + c]], inc=True)
            oacc, oaccB = self.next_pf(2, 4)
            S.pe(f_mm(oacc[:, :], Pm[p2][:, :], vv[sl][:, :], True, False), reads=[PmB[p2], vvB[sl]], writes=[oaccB], inc=False)
            for c in range(2):
                S.pe(f_mm(oacc[:, :], qT[sl][:, c, :], Sb[hp][:, c, :], False, c == 1),
                     reads=[qTB[sl], SbB[hp]], writes=[oaccB], inc=(c == 1))
            qw = self.rcol[:, s, h, 0:1]
            gam = self.rcol[:, s, h, 2:3]
            for c in range(2):
                S.dve(f_stt(Sf[hp][:, c, :], Sf[hp][:, c, :], gam, self.pf[4 + c][:, :], ALU.mult, ALU.add),
                      reads=[SfB[hp], self.pfB[4 + c], self.rtabB], writes=[SfB[hp]])
            S.act(f_acopy(Sb[hp][:, :, :], Sf[hp][:, :, :]), reads=[SfB[hp]], writes=[SbB[hp]])
            if not final:
                S.act(f_amul(o1o[p2][:, :], oacc[:, :], qw), reads=[oaccB, self.rtabB], writes=[o1oB[p2]])
                S.dma("act", f"st_o1{p2}", self.o1_s[t, :, h, :], o1o[p2][:, :], reads=[o1oB[p2]], writes=[self.db("o1", t, h)])
                return None
            p3 = it % 3
            S.dve(f_stt(of[p3][:, :], oacc[:, :], qw, o1i[sl][:, :], ALU.mult, ALU.add),
                  reads=[oaccB, self.rtabB, o1iB[sl]], writes=[ofB[p3]])
            S.act(f_act(junk[:, :], of[p3][:, :], AF.Square, accum_out=fs[p3][:, 0:1]),
                  reads=[ofB[p3]], writes=[junkB, fsB[p3]])

            def fin2():
                S.dve(f_ts(fs[p3][:, 1:2], fs[p3][:, 0:1], 1.0 / 512.0, EPS, ALU.mult, ALU.add),
                      reads=[fsB[p3]], writes=[fsB[p3]])
                S.pool(f_tt(fs[p3][:, 2:3], fs[p3][:, 1:2], self.neghalf[:, :], ALU.pow),
                       reads=[fsB[p3], self.constB], writes=[fsB[p3]])

            def fin3():
                S.dve(f_stt(uu[p2][:, :], of[p3][:, :], fs[p3][:, 2:3], sgi[sl][:, :], ALU.mult, ALU.mult),
                      reads=[ofB[p3], fsB[p3], sgiB[sl]], writes=[uuB[p2]])
                pb, pbb = self.next_pb()
                for c in range(4):
                    S.pe(f_tr(pb[:, c * 128:(c + 1) * 128], uu[p2][:, c * 128:(c + 1) * 128], self.ident_b[:, :]),
                         reads=[uuB[p2], self.identB], writes=[pbb], inc=(c == 3))
                S.act(f_acopy(uTs[p2][:, :, :], pb[:, 0:512].rearrange("p (a b) -> p a b", a=4)),
                      reads=[pbb], writes=[uTsB[p2]])
                S.dma("act", f"st_uT{p2}", self.uT_s[4 * h:4 * h + 4, :, t * 128:(t + 1) * 128].rearrange("c f j -> f c j"),
                      uTs[p2][:, :, :], reads=[uTsB[p2]], writes=[self.db("uT_s")])
            return [fin2, fin3]

        def zero_state(hp):
            S.pool(f_memset(Sf[hp][:, :, :], 0.0), writes=[SfB[hp]])
            S.pool(f_memset(Sb[hp][:, :, :], 0.0), writes=[SbB[hp]])

        def run_chain(chains):
            seq = []
            n = max(len(c) for c in chains)
            for k in range(n):
                for c in chains:
                    if k < len(c):
                        seq.append(c[k])
            slots = {}
            pend = []
            PRE = 2
            for idx in range(min(PRE, len(seq))):
                s, h, t, hp, fin = seq[idx]
                slots[idx] = issue_loads(s, h, t, fin)
            if seq:
                step_a(seq[0][0], seq[0][1], seq[0][2], slots[0])
            for idx, (s, h, t, hp, fin) in enumerate(seq):
                if idx + PRE < len(seq):
                    s2, h2, t2, _, f2 = seq[idx + PRE]
                    slots[idx + PRE] = issue_loads(s2, h2, t2, f2)
                if idx + 1 < len(seq):
                    step_a(seq[idx + 1][0], seq[idx + 1][1], seq[idx + 1][2], slots[idx + 1])
                stages = step(s, h, t, slots[idx], hp, fin)
                newp = []
                for lst in pend:
                    lst.pop(0)()
                    if lst:
                        newp.append(lst)
                pend = newp
                if stages:
                    pend.append(list(stages))
            while pend:
                newp = []
                for lst in pend:
                    lst.pop(0)()
                    if lst:
                        newp.append(lst)
                pend = newp

        for hpair in range(4):
            chains = []
            for hp in range(2):
                h = 2 * hpair + hp
                zero_state(hp)
                chains.append([(0, h, t, hp, False) for t in range(NT)])
            run_chain(chains)
            for hp in range(2):
                h = 2 * hpair + hp
                S.dma("sync", "st_state", self.st_out[h], Sf[hp][:, :, :].rearrange("p c v -> p (c v)"),
                      reads=[SfB[hp]], writes=[self.db("st_out", h)])
                S.op("pool", (lambda hh: (lambda e: e.collective_compute(
                    "AllGather", ALU.bypass, replica_groups=RG, ins=[self.st_out[hh]], outs=[self.st_in[hh]])))(h),
                     reads=[self.db("st_out", h)], writes=[self.db("st_in", h)], chan="cc_st", chan_inc=1)
        for hpair in range(4):
            chains = []
            for hp in range(2):
                h = 2 * hpair + hp
                zero_state(hp)
                chains.append([(1, h, t, hp, True) for t in (1, 0)])
            run_chain(chains)
            chains = []
            for hp in range(2):
                h = 2 * hpair + hp
                for r in range(2):
                    S.dma("sync", f"gs{r}", Gs[r][:, :], self.st_in[h, r * 128:(r + 1) * 128, :],
                          reads=[self.db("st_in", h)], writes=[GsB[r]])
                sfv = Sf[hp][:, :, :].rearrange("p c v -> p (c v)")
                S.dve(f_ts(sfv, Gs[0][:, :], self.sel[:, 0:1], None, ALU.mult),
                      reads=[GsB[0], self.constB], writes=[SfB[hp]])
                S.dve(f_stt(sfv, Gs[1][:, :], self.sel[:, 1:2], sfv, ALU.mult, ALU.add),
                      reads=[GsB[1], self.constB, SfB[hp]], writes=[SfB[hp]])
                S.act(f_acopy(Sb[hp][:, :, :], Sf[hp][:, :, :]), reads=[SfB[hp]], writes=[SbB[hp]])
                chains.append([(1, h, t, hp, True) for t in range(NT - 1, 1, -1)])
            run_chain(chains)

    def att_tables(self, j):
        S = self.S
        tb = self.atabB
        S.dma("sync", "const", self.es[:, :], self.att_sink[j, :].partition_broadcast(128), writes=[tb])
        S.act(f_act(self.es[:, :], self.es[:, :], AF.Exp), reads=[tb], writes=[tb])
        S.dma("sync", "const", self.qkg[:, 0, :], self.att_qg[j, :].partition_broadcast(128), writes=[tb])
        S.dma("sync", "const", self.qkg[:, 1, :], self.att_kg[j, :].partition_broadcast(128), writes=[tb])

    def att_layer(self, i):
        S = self.S
        j = i // 2
        with_ctx = i < 3
        self.att_tables(j)
        self.M.reset()
        M = self.M
        tab, tabB = M.allocn("tab", 2, [128, 256], F32)
        xs, xsB = M.allocn("xs", 2, [128, 512], F32)
        sq, sqB = M.allocn("sq", 2, [128, 512], F32)
        bb, bbB = M.allocn("bb", 2, [128, 512], F32)
        qr, qrB = M.allocn("qr", 2, [128, 512], BF16)
        tpc, tpcB = M.allocn("tpc", 2, [128, 4, 128], BF16)
        vg, vgB = M.allocn("vg", 2, [128, 512], BF16)
        ns, nsB = M.allocn("ns", 2, [128, 12], F32)
        cnt = {"e": 0}

        def epi_qk(kind, gq):
            gi = 0 if kind == "q" else 1

            def epi(t, pf, pfb):
                sl = cnt["e"] % 2
                cnt["e"] += 1
                S.dma("sync", f"tab{sl}", tab[sl][:, :], self.rope_a[t], writes=[tabB[sl]])
                S.act(f_act(sq[sl][:, :], pf[:, :], AF.Square), reads=[pfb], writes=[sqB[sl]])
                S.dve(f_rsum(ns[sl][:, 0:4], sq[sl][:, :].rearrange("p (a b) -> p a b", a=4)), reads=[sqB[sl]], writes=[nsB[sl]])
                S.dve(f_ts(ns[sl][:, 4:8], ns[sl][:, 0:4], 1.0 / 128.0, EPS, ALU.mult, ALU.add),
                      reads=[nsB[sl]], writes=[nsB[sl]])
                S.pool(f_tt(ns[sl][:, 8:12], ns[sl][:, 4:8], self.neghalf[:, 0:1].broadcast_to([128, 4]), ALU.pow),
                       reads=[nsB[sl], self.constB], writes=[nsB[sl]])
                x3 = xs[sl][:, :].rearrange("p (a b) -> p a b", a=4)
                S.dve(f_tt(x3, pf[:, :].rearrange("p (a b) -> p a b", a=4),
                           ns[sl][:, 8:12].unsqueeze(2).broadcast_to([128, 4, 128]), ALU.mult),
                      reads=[pfb, nsB[sl]], writes=[xsB[sl]])
                S.pool(f_tt(x3, x3, self.qkg[:, gi, :].unsqueeze(1).broadcast_to([128, 4, 128]), ALU.mult),
                       reads=[xsB[sl], self.atabB], writes=[xsB[sl]])

                def stage2():
                    xv = xs[sl][:, :].rearrange("p (h a b c) -> p h a b c", h=4, a=2, b=2)
                    bv = bb[sl][:, :].rearrange("p (h a b c) -> p h a b c", h=4, a=2, b=2)
                    sinv = tab[sl][:, 128:256].rearrange("p (a b c) -> p a b c", a=2, b=2)
                    for x12 in range(2):
                        S.dve(f_tt(bv[:, :, :, x12, :], xv[:, :, :, 1 - x12, :],
                                   sinv[:, :, x12, :].unsqueeze(1).broadcast_to([128, 4, 2, 32]), ALU.mult),
                              reads=[xsB[sl], tabB[sl]], writes=[bbB[sl]])
                    S.dve(f_tt(x3, x3, tab[sl][:, 0:128].unsqueeze(1).broadcast_to([128, 4, 128]), ALU.mult),
                          reads=[xsB[sl], tabB[sl]], writes=[xsB[sl]])
                    S.pool(f_tt(qr[sl][:, :], xs[sl][:, :], bb[sl][:, :], ALU.add),
                           reads=[xsB[sl], bbB[sl]], writes=[qrB[sl]])

                def stage3():
                    pb, pbb = self.next_pb()
                    for c in range(4):
                        S.pe(f_tr(pb[:, c * 128:(c + 1) * 128], qr[sl][:, c * 128:(c + 1) * 128], self.ident_b[:, :]),
                             reads=[qrB[sl], self.identB], writes=[pbb], inc=(c == 3))
                    S.act(f_acopy(tpc[sl][:, :, :], pb[:, 0:512].rearrange("p (a b) -> p a b", a=4)),
                          reads=[pbb], writes=[tpcB[sl]])
                    if kind == "q":
                        S.dma("act", f"st_q{sl}", self.aqT_s[t, :, 4 * gq:4 * gq + 4, :], tpc[sl][:, :, :],
                              reads=[tpcB[sl]], writes=[self.db("aqT", t)])
                    else:
                        S.dma("act", f"st_q{sl}", self.akT_s[t], tpc[sl][:, :, :], reads=[tpcB[sl]], writes=[self.db("akT", t)])
                        if t == NT - 1:
                            S.dma("act", f"st_q{sl}", self.halo_out[:, 0:512], tpc[sl][:, :, :].rearrange("p a b -> p (a b)"),
                                  reads=[tpcB[sl]], writes=[self.db("halo_out")])
                return [stage2, stage3]
            return epi

        def epi_v(t, pf, pfb):
            sl = cnt["e"] % 2
            cnt["e"] += 1
            S.act(f_acopy(vg[sl][:, :], pf[:, :]), reads=[pfb], writes=[vgB[sl]])
            S.dma("act", f"st_v{sl}", self.av_s[t].rearrange("p a b -> p (a b)"), vg[sl][:, :], reads=[vgB[sl]],
                  writes=[self.db("av", t)])
            if t == NT - 1:
                S.dma("act", f"st_v{sl}", self.halo_out[:, 512:1024], vg[sl][:, :], reads=[vgB[sl]],
                      writes=[self.db("halo_out")])

        def epi_g(gq):
            def epi(t, pf, pfb):
                sl = cnt["e"] % 2
                cnt["e"] += 1
                S.act(f_act(vg[sl][:, :], pf[:, :], AF.Silu), reads=[pfb], writes=[vgB[sl]])
                S.dma("act", f"st_v{sl}", self.asg_s[t, :, gq * 512:(gq + 1) * 512], vg[sl][:, :], reads=[vgB[sl]],
                      writes=[self.db("asg", t)])
            return epi

        w = self.att_win[j]
        alltiles = list(range(NT))
        qtiles = alltiles if with_ctx else list(range(2, NT))
        groups = [(w[4], alltiles, epi_qk("k", 0)), (w[5], alltiles, epi_v)]
        for gq in range(4):
            groups.append((w[gq], qtiles, epi_qk("q", gq)))
        for gq in range(4):
            groups.append((w[6 + gq], qtiles, epi_g(gq)))
        k_groups, rest = groups[:2], groups[2:]
        self.proj_phase(k_groups)
        S.op("pool", lambda e: e.collective_compute("AllGather", ALU.bypass, replica_groups=RG,
                                                     ins=[self.halo_out.ap()], outs=[self.halo_in.ap()]),
             reads=[self.db("halo_out")], writes=[self.db("halo_in")], chan="cc_halo", chan_inc=1)
        self.proj_phase(rest)
        self.att_mixer(j, qtiles)
        self.outproj_phase(self.att_wout[j], 16, qtiles)

    def att_mixer(self, j, qtiles):
        S = self.S
        self.U.reset()
        self.M.reset()
        U, M = self.U, self.M
        NB = NT + 1
        kTa, kTaB = U.alloc("kTa", [128, NB, 4, 128], BF16)
        Va, VaB = U.alloc("Va", [128, NB, 4, 130], BF16)
        hst, hstB = M.alloc("hst", [128, 2, 1024], BF16)
        qTt, qTtB = U.allocn("qTt", 2, [128, 16, 128], BF16)
        sgt, sgtB = U.allocn("sgt", 2, [128, D], BF16)
        ut, utB = M.allocn("ut", 2, [128, D], BF16)
        uTs, uTsB = M.alloc("uTs", [128, 16, 128], BF16)
        Et, EtB = M.allocn("Et", 3, [128, 512], BF16)
        dn, dnB = M.allocn("dn", 2, [128, 8], F32)
        S.pool(f_memset(Va[:, :, :, 128:130], 1.0), writes=[VaB])
        for t in range(NT):
            S.dma("sync", "akv", kTa[:, t, :, :], self.akT_s[t], reads=[self.db("akT", t)], writes=[kTaB])
            S.dma("sync", "akv", Va[:, t, :, 0:128], self.av_s[t], reads=[self.db("av", t)], writes=[VaB])
        for r in range(2):
            S.dma("sync", "akv", hst[:, r, :], self.halo_in[r * 128:(r + 1) * 128, :],
                  reads=[self.db("halo_in")], writes=[hstB])
        hk = kTa[:, NT, :, :].rearrange("p a b -> p (a b)")
        S.dve(f_ts(hk, hst[:, 0, 0:512], self.sel[:, 0:1], None, ALU.mult), reads=[hstB, self.constB], writes=[kTaB])
        S.dve(f_stt(hk, hst[:, 1, 0:512], self.sel[:, 1:2], hk, ALU.mult, ALU.add),
              reads=[hstB, self.constB, kTaB], writes=[kTaB])
        hv = Va[:, NT, :, 0:128]
        S.dve(f_ts(hv, hst[:, 0, 512:1024].rearrange("p (a b) -> p a b", a=4), self.sel[:, 0:1], None, ALU.mult),
              reads=[hstB, self.constB], writes=[VaB])
        S.dve(f_stt(hv, hst[:, 1, 512:1024].rearrange("p (a b) -> p a b", a=4), self.sel[:, 1:2], hv, ALU.mult, ALU.add),
              reads=[hstB, self.constB, VaB], writes=[VaB])
        scale = 128.0 ** -0.5
        ei = 0
        for qi, t in enumerate(qtiles):
            sl = qi % 2
            S.dma("sync", f"aq{sl}", qTt[sl][:, :, :], self.aqT_s[t], reads=[self.db("aqT", t)], writes=[qTtB[sl]])
            S.dma("sync", f"aq{sl}", sgt[sl][:, :], self.asg_s[t], reads=[self.db("asg", t)], writes=[sgtB[sl]])
            if t < 2:
                blocks = [(0, None), (1, None)]
            else:
                blocks = []
                if t > 2:
                    blocks.append((t - 1, 0))
                blocks.append((t, None))
                blocks.append((t + 1, 1) if t < NT - 1 else (NT, 2))
                blocks += [(0, None), (1, None)]
            for kvh in range(4):
                accs = [self.next_pf(2, 6), self.next_pf(2, 6)]
                def scores(blk):
                    sc_, scB_ = self.next_pf(0, 2)
                    S.pe(f_mm(sc_[:, :], kTa[:, blk, kvh, :], qTt[sl][:, 4 * kvh:4 * kvh + 4, :].rearrange("p a b -> p (a b)"),
                              True, True), reads=[kTaB, qTtB[sl]], writes=[scB_])
                    return sc_, scB_
                nxt = scores(blocks[0][0])
                for bi, (blk, mk) in enumerate(blocks):
                    sc, scB = nxt
                    if bi + 1 < len(blocks):
                        nxt = scores(blocks[bi + 1][0])
                    e3 = ei % 3
                    ei += 1
                    S.act(f_act(Et[e3][:, :], sc[:, :], AF.Exp, scale=scale), reads=[scB], writes=[EtB[e3]])
                    if mk is not None:
                        ev = Et[e3][:, :].rearrange("p (a b) -> p a b", a=4)
                        S.pool(f_tt(ev, ev, self.amask[:, mk, :].unsqueeze(1).broadcast_to([128, 4, 128]), ALU.mult),
                               reads=[EtB[e3], self.constB], writes=[EtB[e3]])
                    for g in range(4):
                        acc, accB = accs[g // 2]
                        S.pe(f_mm(acc[:, (g % 2) * 130:(g % 2) * 130 + 129], Et[e3][:, g * 128:(g + 1) * 128],
                                  Va[:, blk, kvh, 0:129], bi == 0 and g % 2 == 0, bi == len(blocks) - 1),
                             reads=[EtB[e3], VaB], writes=[accB], inc=(g == 3 or bi == len(blocks) - 1))
                d2 = (qi * 4 + kvh) % 2
                for g in range(4):
                    acc, accB = accs[g // 2]
                    h = 4 * kvh + g
                    o0 = (g % 2) * 130
                    S.dve(f_tt(dn[d2][:, g:g + 1], acc[:, o0 + 128:o0 + 129], self.es[:, h:h + 1], ALU.add),
                          reads=[accB, self.atabB], writes=[dnB[d2]])
                S.dve(f_recip(dn[d2][:, 4:8], dn[d2][:, 0:4]), reads=[dnB[d2]], writes=[dnB[d2]])
                for g in range(4):
                    acc, accB = accs[g // 2]
                    h = 4 * kvh + g
                    o0 = (g % 2) * 130
                    S.dve(f_stt(ut[sl][:, h * 128:(h + 1) * 128], acc[:, o0:o0 + 128], dn[d2][:, 4 + g:5 + g],
                                sgt[sl][:, h * 128:(h + 1) * 128], ALU.mult, ALU.mult),
                          reads=[accB, dnB[d2], sgtB[sl]], writes=[utB[sl]])
            for q4 in range(4):
                pb, pbb = self.next_pb()
                for c in range(4):
                    k = q4 * 4 + c
                    S.pe(f_tr(pb[:, c * 128:(c + 1) * 128], ut[sl][:, k * 128:(k + 1) * 128], self.ident_b[:, :]),
                         reads=[utB[sl], self.identB], writes=[pbb], inc=(c == 3))
                S.act(f_acopy(uTs[:, q4 * 4:q4 * 4 + 4, :], pb[:, 0:512].rearrange("p (a b) -> p a b", a=4)),
                      reads=[pbb], writes=[uTsB])
            S.dma("act", "st_uTa", self.uT_s[0:16, :, t * 128:(t + 1) * 128].rearrange("c f j -> f c j"), uTs[:, :, :],
                  reads=[uTsB], writes=[self.db("uT_s")])

    def epilogue(self):
        S = self.S
        yb = Buf("y")
        for t in range(2, NT):
            S.dma("sync", "yout", self.y[t - 2], self.x_sb[:, t, :], reads=[self.xB[t]], writes=[yb])
        if self.dbg:
            for t in range(2):
                S.dma("sync", "yout", self.yctx[t], self.x_sb[:, t, :], reads=[self.xB[t]], writes=[yb])
        S.op("sync", None, reads=[yb], inc=False)

    def emit(self):
        nc, S = self.nc, self.S
        with ExitStack() as es:
            sems = {}
            for e in S.ENG:
                sems[("E", e)] = es.enter_context(nc.semaphore(f"p_{e}"))
            for c in S.chans:
                sems[("D", c)] = es.enter_context(nc.semaphore(f"d_{c}"))
            block = es.enter_context(nc.Block())

            def run(name):
                def body(eng):
                    for waits, fn, rec in S.streams[name]:
                        for k, v in waits:
                            eng.wait_ge(sems[k], v)
                        if fn is None:
                            continue
                        ins = fn(eng)
                        if rec is not None:
                            k = rec[0]
                            if k[0] == "D":
                                amt = S.chans[k[1]][1]
                                if amt == 1:
                                    ins.then_inc(sems[k])
                                else:
                                    ins.then_inc(sems[k], amt)
                            else:
                                ins.then_inc(sems[k], 1)
                return body

            block.sync(run("sync"))
            block.scalar(run("act"))
            block.gpsimd(run("pool"))
            block.vector(run("dve"))
            block.tensor(run("pe"))


def _tile_w(w, ncol):
    K, N = w.shape
    nk = K // 128
    a = w.reshape(nk, 128, N // ncol, ncol).transpose(2, 1, 0, 3)
    return np.ascontiguousarray(a).reshape(N // ncol, 128, nk * ncol)


def _rope_tables(pos, is_ctx, half, n2):
    inv = (10000.0 ** (-np.arange(n2, dtype=np.float32) / np.float32(n2))).astype(np.float32)
    row = (pos // 64).astype(np.float32)
    col = (pos % 64).astype(np.float32)
    outc, outs = [], []
    for p in (row, col):
        ang = (p[:, None] * inv[None, :]).astype(np.float32)
        c, s = np.cos(ang).astype(np.float32), np.sin(ang).astype(np.float32)
        outc += [c, c]
        outs += [-s, s]
    cos2 = np.concatenate(outc, axis=1)
    sin2 = np.concatenate(outs, axis=1)
    cos2[is_ctx] = 1.0
    sin2[is_ctx] = 0.0
    return np.concatenate([cos2, sin2], axis=1).astype(np.float32)


def prep_inputs(inp, layers=(0, 1, 2, 3), cores=range(8)):
    f = lambda a: np.ascontiguousarray(np.asarray(a, dtype=np.float32))
    x, c, ctx, c_ctx = f(inp["x"]), f(inp["c"]), f(inp["ctx"]), f(inp["c_ctx"])
    ada_w, ada_b = f(inp["ada_w"]), f(inp["ada_b"])
    shared = {}
    ret_js = sorted({i // 2 for i in layers if i % 2 == 0})
    att_js = sorted({i // 2 for i in layers if i % 2 == 1})
    for j in ret_js:
        shared[f"ret_win{j}"] = _tile_w(f(inp["ret_w_in"][j]), 512)
        shared[f"ret_wout{j}"] = _tile_w(f(inp["ret_w_out"][j]), 256)
    for j in att_js:
        shared[f"att_win{j}"] = _tile_w(f(inp["attn_w_in"][j]), 512)
        shared[f"att_wout{j}"] = _tile_w(f(inp["attn_w_out"][j]), 512)
    ada_half = {}
    for r in range(2):
        for i in layers:
            ada_half[(r, i)] = _tile_w(ada_w[i][:, r * 3072:(r + 1) * 3072], 512)
    ng = f(inp["norm_gain"])
    shared["ng_cols"] = np.ascontiguousarray(ng.reshape(4, 16, 128).transpose(0, 2, 1))
    shared["ret_gain"] = f(inp["ret_norm_gain"])
    shared["att_qg"] = f(inp["attn_q_gain"])
    shared["att_kg"] = f(inp["attn_k_gain"])
    shared["att_sink"] = f(inp["attn_sink"])
    ii = np.arange(128)
    am = np.stack([(ii[:, None] >= ii[None, :]), (ii[:, None] <= ii[None, :]),
                   (ii[:, None] + ii[None, :] >= 127)], axis=1).astype(np.float32)
    shared["amask"] = np.ascontiguousarray(am.reshape(128, 384))
    shared["ident"] = np.eye(128, dtype=np.float32)
    p = np.arange(128, dtype=np.float32)
    ecol = np.stack([-(p + 1), -(127 - p), np.full(128, -128.0), np.zeros(128),
                     -(128 - p), -p, np.full(128, -128.0), np.zeros(128)], axis=1).astype(np.float32)
    shared["ecol"] = ecol
    erow = np.concatenate([p + 1, 128 - p]).astype(np.float32)
    shared["erow"] = np.ascontiguousarray(np.broadcast_to(erow[None, :], (128, 256)))
    lf, lb = f(inp["ret_decay_logit_fwd"]), f(inp["ret_decay_logit_bwd"])
    maps = []
    for core in cores:
        b, r = core // 2, core % 2
        m = dict(shared)
        if r == 0:
            xl, cl = x[b, :1024], ctx[b]
            pos = np.arange(1024)
        else:
            xl, cl = x[b, 1024:][::-1], ctx[b][::-1]
            pos = np.arange(2047, 1023, -1)
        m["x0"] = np.ascontiguousarray(np.concatenate([cl, xl], axis=0)).reshape(NT, 128, D)
        m["cvec"] = np.stack([c[b], c_ctx], axis=0)
        for i in layers:
            m[f"ada_t{i}"] = ada_half[(r, i)]
        m["ada_bh"] = np.ascontiguousarray(ada_b[:, r * 3072:(r + 1) * 3072])
        lg = np.stack([lf, lb], axis=1) if r == 0 else np.stack([lb, lf], axis=1)
        m["ret_logit"] = np.ascontiguousarray(lg.reshape(2, 16))
        allpos = np.concatenate([np.zeros(256, dtype=np.int64), pos])
        is_ctx = np.arange(TOK) < 256
        m["rope_r"] = _rope_tables(allpos, is_ctx, 128, 64).reshape(NT, 128, 512)
        m["rope_a"] = _rope_tables(allpos, is_ctx, 64, 32).reshape(NT, 128, 256)
        incl1, incl2 = (True, False) if r == 0 else (False, True)
        m1 = (ii[None, :] >= ii[:, None]) if incl1 else (ii[None, :] > ii[:, None])
        m2 = (ii[:, None] >= ii[None, :]) if incl2 else (ii[:, None] > ii[None, :])
        m["rmask"] = np.ascontiguousarray(np.stack([m1, m2], axis=1).astype(np.float32).reshape(128, 256))
        sel = np.zeros((128, 2), np.float32)
        sel[:, 1 - r] = 1.0
        m["sel"] = sel
        maps.append(m)
    return maps


_PROG_CACHE = {}


def kernel(**inputs):
    layers = (0, 1, 2, 3)
    if layers not in _PROG_CACHE:
        _PROG_CACHE[layers] = Prog(layers)
    prog = _PROG_CACHE[layers]
    maps = prep_inputs(inputs, layers)
    res = run_bass_kernel_spmd(prog.nc, maps, core_ids=list(range(8)))
    out = np.empty((4, 2048, D), np.float32)
    for core in range(8):
        b, r = core // 2, core % 2
        y = np.asarray(res.results[core]["y"]).reshape(1024, D)
        if r == 0:
            out[b, :1024] = y
        else:
            out[b, 1024:] = y[::-1]
    return out
```

```python
import math
from contextlib import ExitStack

import numpy as np
import concourse.bass as bass
import concourse.mybir as mybir
from concourse.bass_utils import run_bass_kernel_spmd

F32 = mybir.dt.float32
BF16 = mybir.dt.bfloat16
ALU = mybir.AluOpType
AF = mybir.ActivationFunctionType
AX = mybir.AxisListType

D = 2048
NT = 10
TOK = NT * 128
EPS = 1e-6
RG = [[0, 1], [2, 3], [4, 5], [6, 7]]
SBUF_BASE = 16512
SBUF_END = 229376


class Buf:
    __slots__ = ("w", "r", "name")

    def __init__(self, name=""):
        self.w = None
        self.r = {}
        self.name = name


class Sched:
    ENG = ("sync", "act", "pool", "dve", "pe")

    def __init__(self):
        self.streams = {e: [] for e in self.ENG}
        self.cnt = {e: 0 for e in self.ENG}
        self.waited = {e: {} for e in self.ENG}
        self.chans = {}

    def _val(self, key, val):
        if key[0] == "D":
            c = self.chans[key[1]]
            return c[0] * c[1]
        return val

    def op(self, eng, fn, reads=(), writes=(), chan=None, chan_inc=16, inc=True):
        need = {}

        def add(key, val):
            if not (chan is not None and key == ("D", chan)):
                val = self._val(key, val)
            if key == ("E", "pe") and eng == "pe":
                return
            if need.get(key, 0) < val:
                need[key] = val

        for b in reads:
            if b.w is not None:
                add(*b.w)
        for b in writes:
            if b.w is not None:
                add(*b.w)
            for k, v in b.r.items():
                add(k, v)
        wd = self.waited[eng]
        waits = []
        for k, v in need.items():
            if wd.get(k, 0) < v:
                wd[k] = v
                waits.append((k, v))
        if chan is not None:
            c = self.chans.setdefault(chan, [0, chan_inc])
            c[0] += 1
            ev = (("D", chan), c[0] * c[1])
            rec = ev
        elif inc:
            self.cnt[eng] += 1
            ev = (("E", eng), self.cnt[eng])
            rec = ev
        else:
            ev = (("E", eng), self.cnt[eng] + 1)
            rec = None
        self.streams[eng].append((waits, fn, rec))
        for b in reads:
            if b.r.get(ev[0], 0) < ev[1]:
                b.r[ev[0]] = ev[1]
        for b in writes:
            b.w = ev
            b.r = {}
        return ev

    def dve(self, fn, reads=(), writes=()):
        return self.op("dve", fn, reads, writes)

    def act(self, fn, reads=(), writes=()):
        return self.op("act", fn, reads, writes)

    def pool(self, fn, reads=(), writes=()):
        return self.op("pool", fn, reads, writes)

    def pe(self, fn, reads=(), writes=(), inc=True):
        return self.op("pe", fn, reads, writes, inc=inc)

    def dma(self, q, chan, out, in_, reads=(), writes=(), **kw):
        return self.op(q, lambda e: e.dma_start(out=out, in_=in_, **kw), reads, writes, chan=chan)


def f_mm(out, lhsT, rhs, st, sp):
    return lambda e: e.matmul(out, lhsT, rhs, start=st, stop=sp)


def f_tr(out, in_, ident):
    return lambda e: e.transpose(out, in_, ident)


def f_act(out, in_, func, **kw):
    return lambda e: e.activation(out=out, in_=in_, func=func, **kw)


def f_acopy(out, in_):
    return lambda e: e.copy(out=out, in_=in_)


def f_amul(out, in_, m):
    return lambda e: e.mul(out=out, in_=in_, mul=m)


def f_tt(out, a, b, op):
    return lambda e: e.tensor_tensor(out=out, in0=a, in1=b, op=op)


def f_ts(out, a, s1, s2, op0, op1=None):
    if op1 is None:
        return lambda e: e.tensor_scalar(out=out, in0=a, scalar1=s1, scalar2=None, op0=op0)
    return lambda e: e.tensor_scalar(out=out, in0=a, scalar1=s1, scalar2=s2, op0=op0, op1=op1)


def f_stt(out, a, s, b, op0, op1):
    return lambda e: e.scalar_tensor_tensor(out=out, in0=a, scalar=s, in1=b, op0=op0, op1=op1)


def f_copy(out, in_):
    return lambda e: e.tensor_copy(out=out, in_=in_)


def f_memset(ap, v):
    return lambda e: e.memset(ap, v)


def f_rsum(out, in_):
    return lambda e: e.reduce_sum(out=out, in_=in_, axis=AX.X)


def f_recip(out, in_):
    return lambda e: e.reciprocal(out=out, in_=in_)


DT_SIZE = {F32: 4, BF16: 2}


class Arena:
    def __init__(self, prog, name, base, size):
        self.p = prog
        self.name = name
        self.base = base
        self.size = size
        self.off = 0
        self.gen = 0
        self.bufs = []
        self.pending = {}

    def reset(self):
        pend = dict(self.pending)
        for b in self.bufs:
            if b.w is not None:
                k, v = b.w
                v = self.p.S._val(k, v)
                if pend.get(k, 0) < v:
                    pend[k] = v
            for k, v in b.r.items():
                v = self.p.S._val(k, v)
                if pend.get(k, 0) < v:
                    pend[k] = v
        self.pending = pend
        self.bufs = []
        self.off = 0
        self.gen += 1

    def alloc(self, name, shape, dtype):
        n = 1
        for s in shape[1:]:
            n *= s
        nbytes = (n * DT_SIZE[dtype] + 31) // 32 * 32
        assert self.off + nbytes <= self.size, (self.name, name, self.off, nbytes, self.size)
        h = self.p.nc.alloc_sbuf_tensor_at(f"{self.name}{self.gen}_{name}", list(shape), dtype,
                                           offset=self.base + self.off)
        self.off += nbytes
        b = Buf(name)
        b.r = dict(self.pending)
        self.bufs.append(b)
        return h, b

    def allocn(self, name, n, shape, dtype):
        hs, bs = [], []
        for i in range(n):
            h, b = self.alloc(f"{name}{i}", shape, dtype)
            hs.append(h)
            bs.append(b)
        return hs, bs


class Prog:
    def __init__(self, layers=(0, 1, 2, 3), dbg=False):
        self.layers = tuple(layers)
        self.dbg = dbg
        self.nc = bass.Bass("TRN2", target_bir_lowering=False)
        self.S = Sched()
        self.dram_bufs = {}
        self._sb_off = SBUF_BASE
        self.build()

    def din(self, name, shape, dt=F32):
        return self.nc.dram_tensor(name, list(shape), dt, kind="ExternalInput")

    def dscr(self, name, shape, dt):
        return self.nc.dram_tensor(name, list(shape), dt)

    def sb(self, name, shape, dt):
        n = 1
        for s in shape[1:]:
            n *= s
        nbytes = (n * DT_SIZE[dt] + 31) // 32 * 32
        h = self.nc.alloc_sbuf_tensor_at(name, list(shape), dt, offset=self._sb_off)
        self._sb_off += nbytes
        return h

    def db(self, *key):
        b = self.dram_bufs.get(key)
        if b is None:
            b = Buf(str(key))
            self.dram_bufs[key] = b
        return b

    def build(self):
        nc, S = self.nc, self.S
        L = self.layers
        ret_js = sorted({i // 2 for i in L if i % 2 == 0})
        att_js = sorted({i // 2 for i in L if i % 2 == 1})
        self.x0 = self.din("x0", [NT, 128, D])
        self.cvec = self.din("cvec", [2, D])
        self.ada_t = {i: self.din(f"ada_t{i}", [6, 128, 8192]) for i in L}
        self.ada_bh = self.din("ada_bh", [4, 3072])
        self.ng_cols = self.din("ng_cols", [4, 128, 16])
        self.ret_win = {j: self.din(f"ret_win{j}", [24, 128, 8192]) for j in ret_js}
        self.ret_wout = {j: self.din(f"ret_wout{j}", [8, 128, 8192]) for j in ret_js}
        self.ret_logit = self.din("ret_logit", [2, 16])
        self.ret_gain = self.din("ret_gain", [2, 4096])
        self.att_win = {j: self.din(f"att_win{j}", [10, 128, 8192]) for j in att_js}
        self.att_wout = {j: self.din(f"att_wout{j}", [4, 128, 8192]) for j in att_js}
        self.att_qg = self.din("att_qg", [2, 128])
        self.att_kg = self.din("att_kg", [2, 128])
        self.att_sink = self.din("att_sink", [2, 16])
        self.rope_r = self.din("rope_r", [NT, 128, 512])
        self.rope_a = self.din("rope_a", [NT, 128, 256])
        self.rmask_d = self.din("rmask", [128, 256])
        self.amask_d = self.din("amask", [128, 384])
        self.ecol_d = self.din("ecol", [128, 8])
        self.erow_d = self.din("erow", [128, 256])
        self.sel_d = self.din("sel", [128, 2])
        self.ident_d = self.din("ident", [128, 128])
        self.y = nc.dram_tensor("y", [8, 128, D], F32, kind="ExternalOutput")
        if self.dbg:
            self.yctx = nc.dram_tensor("yctx", [2, 128, D], F32, kind="ExternalOutput")
        self.mod_half = {i: self.dscr(f"mod_half{i}", [2, 3072], F32) for i in L}
        self.mod_full = {i: self.dscr(f"mod_full{i}", [4, 3072], F32) for i in L}
        self.qT_s = self.dscr("qT_s", [NT, 8, 128, 256], BF16)
        self.kT_s = self.dscr("kT_s", [2, NT, 8, 128, 256], BF16)
        self.kk_s = self.dscr("kk_s", [2, NT, 128, 8, 256], BF16)
        self.v_s = self.dscr("v_s", [NT, 128, 8, 512], BF16)
        self.sg_s = self.dscr("sg_s", [NT, 128, 8, 512], BF16)
        self.o1_s = self.dscr("o1_s", [NT, 128, 8, 512], F32)
        self.uT_s = self.dscr("uT_s", [32, 128, TOK], BF16)
        self.st_out = self.dscr("st_out", [8, 128, 1024], F32)
        self.st_in = self.dscr("st_in", [8, 256, 1024], F32)
        self.aqT_s = self.dscr("aqT_s", [NT, 128, 16, 128], BF16)
        self.akT_s = self.dscr("akT_s", [NT, 128, 4, 128], BF16)
        self.av_s = self.dscr("av_s", [NT, 128, 4, 128], BF16)
        self.asg_s = self.dscr("asg_s", [NT, 128, D], BF16)
        self.halo_out = self.dscr("halo_out", [128, 1024], BF16)
        self.halo_in = self.dscr("halo_in", [256, 1024], BF16)
        self.x_sb = self.sb("x_sb", [128, NT, D], F32)
        self.xB = [Buf(f"x{t}") for t in range(NT)]
        self.wbuf = self.sb("wbuf", [128, 2, 8192], BF16)
        self.wB = [Buf("w0"), Buf("w1")]
        self.wslot = 0
        g_base = self._sb_off
        self._sb_off += 2 * D * 4
        self.GA = Arena(self, "GA", g_base, 2 * D * 4)
        self.ngt = self.sb("ngt", [128, 4, 16], F32)
        self.sct = self.sb("sct", [128, 4, 2, 16], F32)
        self.ngB = Buf("ng")
        self.scB = Buf("sc")
        self.ident_f = self.sb("ident_f", [128, 128], F32)
        self.ident_b = self.sb("ident_b", [128, 128], BF16)
        self.identB = Buf("ident")
        self.sel = self.sb("sel", [128, 2], F32)
        self.ecol = self.sb("ecol", [128, 8], F32)
        self.erow = self.sb("erow", [128, 256], F32)
        self.rmask = self.sb("rmask", [128, 2, 128], F32)
        self.amask = self.sb("amask", [128, 3, 128], F32)
        self.constB = Buf("const")
        self.AB = self.sb("AB", [128, 4, 2, 2, 16], F32)
        self.ABB = Buf("AB")
        self.cT_f = self.sb("cT_f", [128, 2, 16], F32)
        self.cT_b = self.sb("cT_b", [128, 16, 2], BF16)
        self.cTB = Buf("cT")
        self.trow = self.sb("trow", [128, 2, 8, 128], F32)
        self.rcol = self.sb("rcol", [128, 2, 8, 4], F32)
        self.lrep = self.sb("lrep", [128, 16], F32)
        self.rtabB = Buf("rtab")
        self.es = self.sb("es", [128, 16], F32)
        self.qkg = self.sb("qkg", [128, 2, 128], F32)
        self.atabB = Buf("atab")
        self.neghalf = self.sb("neghalf", [128, 1], F32)
        used = self._sb_off
        rem = SBUF_END - used
        usz = 40960
        self.U = Arena(self, "U", used, usz)
        self.M = Arena(self, "M", used + usz, rem - usz)
        self.pf = [nc.alloc_psum_tensor(f"pf{i}", [128, 512], F32) for i in range(6)]
        self.pfB = [Buf(f"pf{i}") for i in range(6)]
        self.pb = [nc.alloc_psum_tensor(f"pb{i}", [128, 1024], BF16) for i in range(2)]
        self.pbB = [Buf(f"pb{i}") for i in range(2)]
        self.pf_i = 0
        self.pb_i = 0

        self.prologue()
        for i in L:
            self.layer(i)
        self.epilogue()
        self.emit()

    def next_pf(self, lo=0, hi=6):
        self.pf_i += 1
        i = lo + self.pf_i % (hi - lo)
        return self.pf[i], self.pfB[i]

    def next_pb(self):
        self.pb_i += 1
        i = self.pb_i % 2
        return self.pb[i], self.pbB[i]

    def load_w(self, src_ap):
        s = self.wslot
        self.wslot ^= 1
        self.S.dma("pool", f"w{s}", self.wbuf[:, s, :], src_ap, writes=[self.wB[s]])
        return s

    def prologue(self):
        S, nc = self.S, self.nc
        cb = self.constB
        S.dma("sync", "const", self.ident_f[:, :], self.ident_d[:, :], writes=[self.identB])
        S.dma("sync", "const", self.sel[:, :], self.sel_d[:, :], writes=[cb])
        S.dma("sync", "const", self.ecol[:, :], self.ecol_d[:, :], writes=[cb])
        S.dma("sync", "const", self.erow[:, :], self.erow_d[:, :], writes=[cb])
        S.dma("sync", "const", self.rmask[:, :, :], self.rmask_d.ap().rearrange("p (a b) -> p a b", a=2), writes=[cb])
        S.dma("sync", "const", self.amask[:, :, :], self.amask_d.ap().rearrange("p (a b) -> p a b", a=3), writes=[cb])
        S.act(f_acopy(self.ident_b[:, :], self.ident_f[:, :]), reads=[self.identB], writes=[self.identB])
        S.dve(f_memset(self.neghalf[:, :], -0.5), writes=[cb])
        for t in range(NT):
            S.dma("sync", "xin", self.x_sb[:, t, :], self.x0[t], writes=[self.xB[t]])
        for r in range(2):
            S.dma("sync", "const", self.cT_f[:, r, :], self.cvec[r, :].rearrange("(k p) -> p k", p=128),
                  writes=[self.cTB], allow_slow_non_contiguous=True)
        S.act(f_act(self.cT_b[:, :, :], self.cT_f[:, :, :].rearrange("p r k -> p k r"), AF.Silu),
              reads=[self.cTB], writes=[self.cTB])
        S.dma("sync", "const", self.ngt[:, :, :], self.ng_cols.ap().rearrange("l p k -> p l k"), writes=[self.ngB])
        self.ada_begin(self.layers[0])
        while self.ada_tick():
            pass
        self.ada_finish()

    def ada_begin(self, i):
        self.GA.reset()
        self.ada_bias, self.ada_biasB = self.GA.allocn("adab", 2, [2, 512], F32)
        self.ada_mrow, self.ada_mrowB = self.GA.allocn("mrow", 2, [2, 512], F32)
        self.ada_i = i
        self.ada_g = -1
        self.ada_slots = {}

    def ada_tick(self):
        S = self.S
        i, g = self.ada_i, self.ada_g
        if i is None or g >= 6:
            return False
        if g < 0:
            self.ada_slots[0] = self.load_w(self.ada_t[i][0])
            self.ada_g = 0
            return True
        if g + 1 < 6:
            self.ada_slots[g + 1] = self.load_w(self.ada_t[i][g + 1])
        s = self.ada_slots[g]
        sl = g % 2
        S.dma("sync", f"adab{sl}", self.ada_bias[sl][:, :],
              self.ada_bh[i, g * 512:(g + 1) * 512].partition_broadcast(2), writes=[self.ada_biasB[sl]])
        pf, pfb = self.pf[0], self.pfB[0]
        for k in range(16):
            S.pe(f_mm(pf[0:2, :], self.cT_b[:, k, :], self.wbuf[:, s, k * 512:(k + 1) * 512], k == 0, k == 15),
                 reads=[self.cTB, self.wB[s]], writes=[pfb], inc=(k == 15))
        S.dve(f_tt(self.ada_mrow[sl][:, :], pf[0:2, :], self.ada_bias[sl][:, :], ALU.add),
              reads=[pfb, self.ada_biasB[sl]], writes=[self.ada_mrowB[sl]])
        S.dma("sync", f"modst{sl}", self.mod_half[i][:, g * 512:(g + 1) * 512], self.ada_mrow[sl][:, :],
              reads=[self.ada_mrowB[sl]], writes=[self.db("mod_half", i)])
        self.ada_g += 1
        if self.ada_g == 6:
            S.op("pool", (lambda ii: (lambda e: e.collective_compute(
                "AllGather", ALU.bypass, replica_groups=RG, ins=[self.mod_half[ii].ap()], outs=[self.mod_full[ii].ap()])))(i),
                 reads=[self.db("mod_half", i)], writes=[self.db("mod_full", i)], chan="cc_mod", chan_inc=1)
        return True

    def ada_finish(self):
        S = self.S
        i = self.ada_i
        if i is None:
            return
        while self.ada_tick():
            pass
        mf = self.mod_full[i]
        kw = dict(allow_slow_non_contiguous=True)
        for r in range(2):
            S.dma("sync", "const", self.AB[:, i, r, 1, :], mf[r, 0:2048].rearrange("(k p) -> p k", p=128),
                  reads=[self.db("mod_full", i)], writes=[self.ABB], **kw)
            S.dma("sync", "const", self.sct[:, i, r, 0:8], mf[r, 2048:3072].rearrange("(k p) -> p k", p=128),
                  reads=[self.db("mod_full", i)], writes=[self.scB], **kw)
            S.dma("sync", "const", self.sct[:, i, r, 8:16], mf[2 + r, 0:1024].rearrange("(k p) -> p k", p=128),
                  reads=[self.db("mod_full", i)], writes=[self.scB], **kw)
        S.dve(f_ts(self.sct[:, i, :, :], self.sct[:, i, :, :], 1.0, math.sqrt(D), ALU.add, ALU.mult),
              reads=[self.scB], writes=[self.scB])
        for r in range(2):
            S.dve(f_tt(self.AB[:, i, r, 0, :], self.sct[:, i, r, :], self.ngt[:, i, :], ALU.mult),
                  reads=[self.scB, self.ngB, self.ABB], writes=[self.ABB])
        self.ada_i = None

    def load_gate(self, i):
        S = self.S
        self.GA.reset()
        self.G, self.GB = self.GA.alloc("G", [128, 2, D], F32)
        for r in range(2):
            S.dma("sync", "gate", self.G[:, r, :], self.mod_full[i][2 + r, 1024:3072].partition_broadcast(128),
                  reads=[self.db("mod_full", i)], writes=[self.GB])

    def norm_phase(self, i):
        S = self.S
        self.U.reset()
        self.M.reset()
        hT, hTB0 = self.U.alloc("hT", [128, 16, TOK], BF16)
        self.hT = hT
        self.hTB = [Buf(f"hT{t}") for t in range(NT)]
        for b in self.hTB:
            b.r = dict(hTB0.r)
        self.U.bufs.extend(self.hTB)
        xn, xnB = self.M.allocn("xn", 2, [128, D], F32)
        ss, ssB = self.M.allocn("nss", 3, [128, 4], F32)
        pend = []

        def advance():
            nonlocal pend
            newp = []
            for lst in pend:
                lst.pop(0)()
                if lst:
                    newp.append(lst)
            pend = newp

        def tile_stages(t):
            r = 1 if t < 2 else 0
            sl = t % 2
            s3 = t % 3

            def st1():
                S.act(f_act(xn[sl][:, :], self.x_sb[:, t, :], AF.Square, accum_out=ss[s3][:, 0:1]),
                      reads=[self.xB[t]], writes=[xnB[sl], ssB[s3]])
                S.dve(f_ts(ss[s3][:, 1:2], ss[s3][:, 0:1], float(D * EPS), None, ALU.add), reads=[ssB[s3]], writes=[ssB[s3]])
                S.pool(f_tt(ss[s3][:, 2:3], ss[s3][:, 1:2], self.neghalf[:, :], ALU.pow),
                       reads=[ssB[s3], self.constB], writes=[ssB[s3]])

            banks = []

            def st2():
                S.dve(f_ts(xn[sl][:, :], self.x_sb[:, t, :], ss[s3][:, 2:3], None, ALU.mult),
                      reads=[self.xB[t], ssB[s3]], writes=[xnB[sl]])
                for q4 in range(4):
                    pf, pfb = self.next_pf()
                    banks.append((pf, pfb))
                    for kk in range(4):
                        k = q4 * 4 + kk
                        S.pe(f_tr(pf[:, kk * 128:(kk + 1) * 128], xn[sl][:, k * 128:(k + 1) * 128], self.ident_f[:, :]),
                             reads=[xnB[sl], self.identB], writes=[pfb], inc=(kk == 3))

            def st3():
                for q4 in range(4):
                    pf, pfb = banks[q4]
                    for kk in range(4):
                        k = q4 * 4 + kk
                        src = pf[:, kk * 128:(kk + 1) * 128]
                        dst = hT[:, k, t * 128:(t + 1) * 128]
                        a_c = self.AB[:, i, r, 0, k:k + 1]
                        b_c = self.AB[:, i, r, 1, k:k + 1]
                        if kk == 3:
                            S.act(f_act(dst, src, AF.Identity, scale=a_c, bias=b_c),
                                  reads=[pfb, self.ABB], writes=[self.hTB[t]])
                        else:
                            S.dve(f_ts(dst, src, a_c, b_c, ALU.mult, ALU.add),
                                  reads=[pfb, self.ABB], writes=[self.hTB[t]])
            return [st1, st2, st3]

        for t in range(NT):
            pend.append(tile_stages(t))
            advance()
        while pend:
            advance()

    def proj_phase(self, groups):
        S = self.S
        slots = [None] * len(groups)
        if groups:
            slots[0] = self.load_w(groups[0][0])
        pend = []

        def advance():
            nonlocal pend
            newp = []
            for lst in pend:
                lst.pop(0)()
                if lst:
                    newp.append(lst)
            pend = newp

        for gi, (w_ap, tiles, epi) in enumerate(groups):
            if gi + 1 < len(groups):
                slots[gi + 1] = self.load_w(groups[gi + 1][0])
            s = slots[gi]
            for t in tiles:
                pf, pfb = self.next_pf(0, 4)
                for k in range(16):
                    S.pe(f_mm(pf[:, :], self.hT[:, k, t * 128:(t + 1) * 128], self.wbuf[:, s, k * 512:(k + 1) * 512],
                              k == 0, k == 15),
                         reads=[self.hTB[t], self.wB[s]], writes=[pfb], inc=(k == 15))
                stages = epi(t, pf, pfb)
                advance()
                if stages:
                    pend.append(list(stages))
        while pend:
            advance()

    def outproj_phase(self, w_dram, nk, tiles, final_store=False):
        S = self.S
        ncol = 8192 // nk
        ng = D // ncol
        nst = 1 if nk == 16 else 2
        per = (len(tiles) + nst - 1) // nst
        self.U.reset()
        self.M.reset()
        uT, uTB = self.U.alloc("uT", [128, nk, 20480 // nk], BF16)
        tmp, tmpB = self.M.allocn("otmp", 2, [128, 512], F32)
        ti = 0
        for st in range(nst):
            tl = tiles[st * per:(st + 1) * per]
            if not tl:
                continue
            t0, ntok = tl[0], len(tl) * 128
            S.dma("sync", "uTld", uT[:, :, 0:ntok], self.uT_s[0:nk, :, t0 * 128:t0 * 128 + ntok].rearrange("c f j -> f c j"),
                  reads=[self.db("uT_s")], writes=[uTB])
            slots = [None] * ng
            slots[0] = self.load_w(w_dram[0])
            for g in range(ng):
                if g + 1 < ng:
                    slots[g + 1] = self.load_w(w_dram[g + 1])
                s = slots[g]
                for li, t in enumerate(tl):
                    pf, pfb = self.next_pf(0, 4)
                    for k in range(nk):
                        S.pe(f_mm(pf[:, 0:ncol], uT[:, k, li * 128:(li + 1) * 128],
                                  self.wbuf[:, s, k * ncol:(k + 1) * ncol], k == 0, k == nk - 1),
                             reads=[uTB, self.wB[s]], writes=[pfb], inc=(k == nk - 1))
                    r = 1 if t < 2 else 0
                    s2 = ti % 2
                    ti += 1
                    cs = slice(g * ncol, (g + 1) * ncol)
                    S.dve(f_tt(tmp[s2][:, 0:ncol], pf[:, 0:ncol], self.G[:, r, cs], ALU.mult),
                          reads=[pfb, self.GB], writes=[tmpB[s2]])
                    S.pool(f_tt(self.x_sb[:, t, cs], self.x_sb[:, t, cs], tmp[s2][:, 0:ncol], ALU.add),
                           reads=[tmpB[s2], self.xB[t]], writes=[self.xB[t]])

    def layer(self, i):
        self.cur_layer = i
        self.norm_phase(i)
        idx = self.layers.index(i)
        if idx + 1 < len(self.layers):
            self.ada_begin(self.layers[idx + 1])
        else:
            self.ada_i = None
        if i % 2 == 0:
            self.ret_layer(i)
        else:
            self.att_layer(i)

    def ret_tables(self, j):
        S = self.S
        tb = self.rtabB
        S.dma("sync", "const", self.lrep[:, :], self.ret_logit[j, :].partition_broadcast(128), writes=[tb])
        S.act(f_act(self.lrep[:, :], self.lrep[:, :], AF.Exp, scale=-1.0), reads=[tb], writes=[tb])
        S.act(f_act(self.lrep[:, :], self.lrep[:, :], AF.Ln, bias=1.0), reads=[tb], writes=[tb])
        for s in range(2):
            for h in range(8):
                lc = self.lrep[:, s * 8 + h:s * 8 + h + 1]
                S.act(f_act(self.trow[:, s, h, :], self.erow[:, s * 128:(s + 1) * 128], AF.Exp, scale=lc),
                      reads=[tb, self.constB], writes=[tb])
                S.act(f_act(self.rcol[:, s, h, :], self.ecol[:, s * 4:(s + 1) * 4], AF.Exp, scale=lc),
                      reads=[tb, self.constB], writes=[tb])

    def ret_layer(self, i):
        S = self.S
        j = i // 2
        self.ret_tables(j)
        self.M.reset()
        M = self.M
        tab, tabB = M.allocn("tab", 2, [128, 512], F32)
        xs, xsB = M.allocn("xs", 2, [128, 512], F32)
        bb, bbB = M.allocn("bb", 2, [128, 512], F32)
        qr, qrB = M.allocn("qr", 2, [128, 512], BF16)
        tpc2, tpcB2 = M.allocn("tpc", 2, [128, 4, 128], BF16)
        kT12, kT12B = M.alloc("kT12", [128, 2, 4, 128], BF16)
        k12, k12B = M.alloc("k12", [128, 2, 512], BF16)
        vg, vgB = M.allocn("vg", 2, [128, 512], BF16)
        gn, gnB = M.alloc("gn", [128, 512], F32)
        cnt = {"e": 0}

        def epi_qk(kind, gq):
            def epi(t, pf, pfb):
                sl = cnt["e"] % 2
                cnt["e"] += 1
                tpc, tpcB = tpc2[sl], tpcB2[sl]
                S.dma("sync", f"tab{sl}", tab[sl][:, :], self.rope_r[t], writes=[tabB[sl]])
                S.act(f_act(xs[sl][:, :], pf[:, :], AF.Copy, scale=(1.0 / 16.0 if kind == "q" else 1.0)),
                      reads=[pfb], writes=[xsB[sl]])

                def stage2():
                    xv = xs[sl][:, :].rearrange("p (h a b c) -> p h a b c", h=2, a=2, b=2)
                    bv = bb[sl][:, :].rearrange("p (h a b c) -> p h a b c", h=2, a=2, b=2)
                    cos_bc = tab[sl][:, 0:256].unsqueeze(1).broadcast_to([128, 2, 256])
                    sinv = tab[sl][:, 256:512].rearrange("p (a b c) -> p a b c", a=2, b=2)
                    for x12 in range(2):
                        S.dve(f_tt(bv[:, :, :, x12, :], xv[:, :, :, 1 - x12, :],
                                   sinv[:, :, x12, :].unsqueeze(1).broadcast_to([128, 2, 2, 64]), ALU.mult),
                              reads=[xsB[sl], tabB[sl]], writes=[bbB[sl]])
                    S.dve(f_tt(xs[sl][:, :].rearrange("p (a b) -> p a b", a=2),
                               xs[sl][:, :].rearrange("p (a b) -> p a b", a=2), cos_bc, ALU.mult),
                          reads=[xsB[sl], tabB[sl]], writes=[xsB[sl]])
                    (S.pool if kind == "q" else S.dve)(f_tt(qr[sl][:, :], xs[sl][:, :], bb[sl][:, :], ALU.add),
                                                       reads=[xsB[sl], bbB[sl]], writes=[qrB[sl]])

                def stage3():
                    pb, pbb = self.next_pb()
                    for c in range(4):
                        S.pe(f_tr(pb[:, c * 128:(c + 1) * 128], qr[sl][:, c * 128:(c + 1) * 128], self.ident_b[:, :]),
                             reads=[qrB[sl], self.identB], writes=[pbb], inc=(c == 3))
                    S.act(f_acopy(tpc[:, :, :], pb[:, 0:512].rearrange("p (a b) -> p a b", a=4)), reads=[pbb], writes=[tpcB])
                    if kind == "q":
                        S.dma("act", f"st_q{sl}", self.qT_s[t, 2 * gq:2 * gq + 2].rearrange("h d x -> d h x"),
                              tpc[:, :, :].rearrange("p (h c) j -> p h (c j)", h=2),
                              reads=[tpcB], writes=[self.db("qT", t)])
                        return
                    for s in range(2):
                        S.pool(f_tt(kT12[:, s, :, :].rearrange("p (h c) j -> p h c j", h=2),
                                    tpc[:, :, :].rearrange("p (h c) j -> p h c j", h=2),
                                    self.trow[:, s, 2 * gq:2 * gq + 2, :].unsqueeze(2).broadcast_to([128, 2, 2, 128]), ALU.mult),
                               reads=[tpcB, self.rtabB], writes=[kT12B])
                        for hh in range(2):
                            h = 2 * gq + hh
                            S.act(f_amul(k12[:, s, hh * 256:(hh + 1) * 256], qr[sl][:, hh * 256:(hh + 1) * 256],
                                         self.rcol[:, s, h, 1:2]),
                                  reads=[qrB[sl], self.rtabB], writes=[k12B])
                        S.dma("sync", "st_kT", self.kT_s[s, t, 2 * gq:2 * gq + 2].rearrange("h d x -> d h x"),
                              kT12[:, s, :, :].rearrange("p (h c) j -> p h (c j)", h=2),
                              reads=[kT12B], writes=[self.db("kT", t)])
                        S.dma("act", "st_kk", self.kk_s[s, t, :, 2 * gq:2 * gq + 2, :],
                              k12[:, s, :].rearrange("p (h x) -> p h x", h=2),
                              reads=[k12B], writes=[self.db("kk", t)])
                return [stage2, stage3]
            return epi

        def epi_v(h):
            def epi(t, pf, pfb):
                sl = cnt["e"] % 2
                cnt["e"] += 1
                S.act(f_acopy(vg[sl][:, :], pf[:, :]), reads=[pfb], writes=[vgB[sl]])
                S.dma("act", f"st_v{sl}", self.v_s[t, :, h, :], vg[sl][:, :], reads=[vgB[sl]], writes=[self.db("v", t)])
            return epi

        def epi_g(h):
            def epi(t, pf, pfb):
                sl = cnt["e"] % 2
                cnt["e"] += 1
                if t == 0:
                    S.dma("sync", "gn", gn[:, :], self.ret_gain[j, h * 512:(h + 1) * 512].partition_broadcast(128),
                          writes=[gnB])
                S.act(f_act(xs[sl][:, :], pf[:, :], AF.Silu), reads=[pfb], writes=[xsB[sl]])
                S.dve(f_tt(vg[sl][:, :], xs[sl][:, :], gn[:, :], ALU.mult), reads=[xsB[sl], gnB], writes=[vgB[sl]])
                S.dma("sync", f"st_v{sl}", self.sg_s[t, :, h, :], vg[sl][:, :], reads=[vgB[sl]], writes=[self.db("sg", t)])
            return epi

        w = self.ret_win[j]
        alltiles = list(range(NT))
        groups = []
        for gq in range(4):
            groups.append((w[gq], alltiles, epi_qk("q", gq)))
        for gq in range(4):
            groups.append((w[4 + gq], alltiles, epi_qk("k", gq)))
        for h in range(8):
            groups.append((w[8 + h], alltiles, epi_v(h)))
        for h in range(8):
            groups.append((w[16 + h], alltiles, epi_g(h)))
        self.proj_phase(groups)
        self.ret_mixer(j)
        self.ada_finish()
        self.load_gate(i)
        self.outproj_phase(self.ret_wout[j], 32, list(range(NT)))

    def ret_mixer(self, j):
        S = self.S
        self.U.reset()
        self.M.reset()
        U, M = self.U, self.M
        Sf, SfB = U.allocn("Sf", 2, [128, 2, 512], F32)
        Sb, SbB = U.allocn("Sb", 2, [128, 2, 512], BF16)
        Gs, GsB = U.allocn("Gs", 2, [128, 1024], F32)
        NS = 5
        qT, qTB = U.allocn("qT", NS, [128, 2, 128], BF16)
        kT, kTB = U.allocn("kT", NS, [128, 2, 128], BF16)
        kk, kkB = U.allocn("kk", NS, [128, 256], BF16)
        vv, vvB = U.allocn("vv", NS, [128, 512], BF16)
        Pm, PmB = U.allocn("Pm", 2, [128, 128], BF16)
        o1o, o1oB = U.allocn("o1o", 2, [128, 512], F32)
        o1i, o1iB = M.allocn("o1i", NS, [128, 512], F32)
        sgi, sgiB = M.allocn("sgi", NS, [128, 512], BF16)
        of, ofB = M.allocn("of", 3, [128, 512], F32)
        junk, junkB = U.alloc("junk", [128, 512], BF16)
        uu, uuB = M.allocn("uu", 2, [128, 512], BF16)
        uTs, uTsB = U.allocn("uTs", 2, [128, 4, 128], BF16)
        fs, fsB = M.allocn("fs", 3, [128, 4], F32)
        st = {"ld": 0, "it": 0, "ita": 0}

        def issue_loads(s, h, t, final=False):
            sl = st["ld"] % NS
            st["ld"] += 1
            if final:
                S.dma("sync", f"rq{sl}", o1i[sl][:, :], self.o1_s[t, :, h, :], reads=[self.db("o1", t, h)], writes=[o1iB[sl]])
                S.dma("sync", f"rq{sl}", sgi[sl][:, :], self.sg_s[t, :, h, :], reads=[self.db("sg", t)], writes=[sgiB[sl]])
            S.dma("sync", f"rq{sl}", qT[sl][:, :, :], self.qT_s[t, h].rearrange("d (c j) -> d c j", c=2),
                  reads=[self.db("qT", t)], writes=[qTB[sl]])
            S.dma("sync", f"rq{sl}", kT[sl][:, :, :], self.kT_s[s, t, h].rearrange("d (c j) -> d c j", c=2),
                  reads=[self.db("kT", t)], writes=[kTB[sl]])
            S.dma("sync", f"rq{sl}", kk[sl][:, :], self.kk_s[s, t, :, h, :], reads=[self.db("kk", t)], writes=[kkB[sl]])
            S.dma("sync", f"rq{sl}", vv[sl][:, :], self.v_s[t, :, h, :], reads=[self.db("v", t)], writes=[vvB[sl]])
            return sl

        def step_a(s, h, t, sl):
            it = st["ita"]
            st["ita"] += 1
            p2 = it % 2
            sc, scB = self.next_pf(0, 2)
            for c in range(2):
                S.pe(f_mm(sc[:, 0:128], kT[sl][:, c, :], qT[sl][:, c, :], c == 0, c == 1),
                     reads=[kTB[sl], qTB[sl]], writes=[scB], inc=(c == 1))
            S.dve(f_tt(Pm[p2][:, :], sc[:, 0:128], self.rmask[:, s, :], ALU.mult),
                  reads=[scB, self.constB], writes=[PmB[p2]])

        def step(s, h, t, sl, hp, final):
            it = st["it"]
            st["it"] += 1
            p2 = it % 2
            for c in range(2):
                S.pe(f_mm(self.pf[4 + c][:, :], kk[sl][:, c * 128:(c + 1) * 128], vv[sl][:, :], True, True),
                     reads=[kkB[sl], vvB[sl]], writes=[self.pfB[4 + c]], inc=True)
            oacc, oaccB = self.next_pf(2, 4)
            S.pe(f_mm(oacc[:, :], Pm[p2][:, :], vv[sl][:, :], True, False), reads=[PmB[p2], vvB[sl]], writes=[oaccB], inc=False)
            for c in range(2):
                S.pe(f_mm(oacc[:, :], qT[sl][:, c, :], Sb[hp][:, c, :], False, c == 1),
                     reads=[qTB[sl], SbB[hp]], writes=[oaccB], inc=(c == 1))
            qw = self.rcol[:, s, h, 0:1]
            gam = self.rcol[:, s, h, 2:3]
            for c in range(2):
                S.dve(f_stt(Sf[hp][:, c, :], Sf[hp][:, c, :], gam, self.pf[4 + c][:, :], ALU.mult, ALU.add),
                      reads=[SfB[hp], self.pfB[4 + c], self.rtabB], writes=[SfB[hp]])
            S.act(f_acopy(Sb[hp][:, :, :], Sf[hp][:, :, :]), reads=[SfB[hp]], writes=[SbB[hp]])
            if not final:
                S.act(f_amul(o1o[p2][:, :], oacc[:, :], qw), reads=[oaccB, self.rtabB], writes=[o1oB[p2]])
                S.dma("act", f"st_o1{p2}", self.o1_s[t, :, h, :], o1o[p2][:, :], reads=[o1oB[p2]], writes=[self.db("o1", t, h)])
                return None
            p3 = it % 3
            S.dve(f_stt(of[p3][:, :], oacc[:, :], qw, o1i[sl][:, :], ALU.mult, ALU.add),
                  reads=[oaccB, self.rtabB, o1iB[sl]], writes=[ofB[p3]])
            S.act(f_act(junk[:, :], of[p3][:, :], AF.Square, accum_out=fs[p3][:, 0:1]),
                  reads=[ofB[p3]], writes=[junkB, fsB[p3]])

            def fin2():
                S.dve(f_ts(fs[p3][:, 1:2], fs[p3][:, 0:1], 1.0 / 512.0, EPS, ALU.mult, ALU.add),
                      reads=[fsB[p3]], writes=[fsB[p3]])
                S.pool(f_tt(fs[p3][:, 2:3], fs[p3][:, 1:2], self.neghalf[:, :], ALU.pow),
                       reads=[fsB[p3], self.constB], writes=[fsB[p3]])

            def fin3():
                S.dve(f_stt(uu[p2][:, :], of[p3][:, :], fs[p3][:, 2:3], sgi[sl][:, :], ALU.mult, ALU.mult),
                      reads=[ofB[p3], fsB[p3], sgiB[sl]], writes=[uuB[p2]])
                pb, pbb = self.next_pb()
                for c in range(4):
                    S.pe(f_tr(pb[:, c * 128:(c + 1) * 128], uu[p2][:, c * 128:(c + 1) * 128], self.ident_b[:, :]),
                         reads=[uuB[p2], self.identB], writes=[pbb], inc=(c == 3))
                S.act(f_acopy(uTs[p2][:, :, :], pb[:, 0:512].rearrange("p (a b) -> p a b", a=4)),
                      reads=[pbb], writes=[uTsB[p2]])
                S.dma("act", f"st_uT{p2}", self.uT_s[4 * h:4 * h + 4, :, t * 128:(t + 1) * 128].rearrange("c f j -> f c j"),
                      uTs[p2][:, :, :], reads=[uTsB[p2]], writes=[self.db("uT_s")])
            return [fin2, fin3]

        def zero_state(hp):
            S.pool(f_memset(Sf[hp][:, :, :], 0.0), writes=[SfB[hp]])
            S.pool(f_memset(Sb[hp][:, :, :], 0.0), writes=[SbB[hp]])

        def run_chain(chains):
            seq = []
            n = max(len(c) for c in chains)
            for k in range(n):
                for c in chains:
                    if k < len(c):
                        seq.append(c[k])
            slots = {}
            pend = []
            PRE = 2
            for idx in range(min(PRE, len(seq))):
                s, h, t, hp, fin = seq[idx]
                slots[idx] = issue_loads(s, h, t, fin)
            if seq:
                step_a(seq[0][0], seq[0][1], seq[0][2], slots[0])
            for idx, (s, h, t, hp, fin) in enumerate(seq):
                if idx + PRE < len(seq):
                    s2, h2, t2, _, f2 = seq[idx + PRE]
                    slots[idx + PRE] = issue_loads(s2, h2, t2, f2)
                if idx + 1 < len(seq):
                    step_a(seq[idx + 1][0], seq[idx + 1][1], seq[idx + 1][2], slots[idx + 1])
                stages = step(s, h, t, slots[idx], hp, fin)
                newp = []
                for lst in pend:
                    lst.pop(0)()
                    if lst:
                        newp.append(lst)
                pend = newp
                if stages:
                    pend.append(list(stages))
            while pend:
                newp = []
                for lst in pend:
                    lst.pop(0)()
                    if lst:
                        newp.append(lst)
                pend = newp

        for hpair in range(4):
            chains = []
            for hp in range(2):
                h = 2 * hpair + hp
                zero_state(hp)
                chains.append([(0, h, t, hp, False) for t in range(NT)])
            run_chain(chains)
            self.ada_tick()
            for hp in range(2):
                h = 2 * hpair + hp
                S.dma("sync", "st_state", self.st_out[h], Sf[hp][:, :, :].rearrange("p c v -> p (c v)"),
                      reads=[SfB[hp]], writes=[self.db("st_out", h)])
                S.op("pool", (lambda hh: (lambda e: e.collective_compute(
                    "AllGather", ALU.bypass, replica_groups=RG, ins=[self.st_out[hh]], outs=[self.st_in[hh]])))(h),
                     reads=[self.db("st_out", h)], writes=[self.db("st_in", h)], chan="cc_st", chan_inc=1)
        for hpair in range(4):
            chains = []
            for hp in range(2):
                h = 2 * hpair + hp
                zero_state(hp)
                chains.append([(1, h, t, hp, True) for t in (1, 0)])
            run_chain(chains)
            self.ada_tick()
            chains = []
            for hp in range(2):
                h = 2 * hpair + hp
                for r in range(2):
                    S.dma("sync", f"gs{r}", Gs[r][:, :], self.st_in[h, r * 128:(r + 1) * 128, :],
                          reads=[self.db("st_in", h)], writes=[GsB[r]])
                sfv = Sf[hp][:, :, :].rearrange("p c v -> p (c v)")
                S.dve(f_ts(sfv, Gs[0][:, :], self.sel[:, 0:1], None, ALU.mult),
                      reads=[GsB[0], self.constB], writes=[SfB[hp]])
                S.dve(f_stt(sfv, Gs[1][:, :], self.sel[:, 1:2], sfv, ALU.mult, ALU.add),
                      reads=[GsB[1], self.constB, SfB[hp]], writes=[SfB[hp]])
                S.act(f_acopy(Sb[hp][:, :, :], Sf[hp][:, :, :]), reads=[SfB[hp]], writes=[SbB[hp]])
                chains.append([(1, h, t, hp, True) for t in range(NT - 1, 1, -1)])
            run_chain(chains)

    def att_tables(self, j):
        S = self.S
        tb = self.atabB
        S.dma("sync", "const", self.es[:, :], self.att_sink[j, :].partition_broadcast(128), writes=[tb])
        S.act(f_act(self.es[:, :], self.es[:, :], AF.Exp), reads=[tb], writes=[tb])
        S.dma("sync", "const", self.qkg[:, 0, :], self.att_qg[j, :].partition_broadcast(128), writes=[tb])
        S.dma("sync", "const", self.qkg[:, 1, :], self.att_kg[j, :].partition_broadcast(128), writes=[tb])

    def att_layer(self, i):
        S = self.S
        j = i // 2
        with_ctx = i < 3
        self.att_tables(j)
        self.M.reset()
        M = self.M
        tab, tabB = M.allocn("tab", 2, [128, 256], F32)
        xs, xsB = M.allocn("xs", 2, [128, 512], F32)
        sq, sqB = M.allocn("sq", 2, [128, 512], F32)
        bb, bbB = M.allocn("bb", 2, [128, 512], F32)
        qr, qrB = M.allocn("qr", 2, [128, 512], BF16)
        tpc, tpcB = M.allocn("tpc", 2, [128, 4, 128], BF16)
        vg, vgB = M.allocn("vg", 2, [128, 512], BF16)
        ns, nsB = M.allocn("ns", 2, [128, 12], F32)
        cnt = {"e": 0}

        def epi_qk(kind, gq):
            gi = 0 if kind == "q" else 1

            def epi(t, pf, pfb):
                sl = cnt["e"] % 2
                cnt["e"] += 1
                S.dma("sync", f"tab{sl}", tab[sl][:, :], self.rope_a[t], writes=[tabB[sl]])
                S.act(f_act(sq[sl][:, :], pf[:, :], AF.Square), reads=[pfb], writes=[sqB[sl]])
                S.dve(f_rsum(ns[sl][:, 0:4], sq[sl][:, :].rearrange("p (a b) -> p a b", a=4)), reads=[sqB[sl]], writes=[nsB[sl]])
                S.dve(f_ts(ns[sl][:, 4:8], ns[sl][:, 0:4], 1.0 / 128.0, EPS, ALU.mult, ALU.add),
                      reads=[nsB[sl]], writes=[nsB[sl]])
                S.pool(f_tt(ns[sl][:, 8:12], ns[sl][:, 4:8], self.neghalf[:, 0:1].broadcast_to([128, 4]), ALU.pow),
                       reads=[nsB[sl], self.constB], writes=[nsB[sl]])
                x3 = xs[sl][:, :].rearrange("p (a b) -> p a b", a=4)
                S.dve(f_tt(x3, pf[:, :].rearrange("p (a b) -> p a b", a=4),
                           ns[sl][:, 8:12].unsqueeze(2).broadcast_to([128, 4, 128]), ALU.mult),
                      reads=[pfb, nsB[sl]], writes=[xsB[sl]])
                S.pool(f_tt(x3, x3, self.qkg[:, gi, :].unsqueeze(1).broadcast_to([128, 4, 128]), ALU.mult),
                       reads=[xsB[sl], self.atabB], writes=[xsB[sl]])

                def stage2():
                    xv = xs[sl][:, :].rearrange("p (h a b c) -> p h a b c", h=4, a=2, b=2)
                    bv = bb[sl][:, :].rearrange("p (h a b c) -> p h a b c", h=4, a=2, b=2)
                    sinv = tab[sl][:, 128:256].rearrange("p (a b c) -> p a b c", a=2, b=2)
                    for x12 in range(2):
                        S.dve(f_tt(bv[:, :, :, x12, :], xv[:, :, :, 1 - x12, :],
                                   sinv[:, :, x12, :].unsqueeze(1).broadcast_to([128, 4, 2, 32]), ALU.mult),
                              reads=[xsB[sl], tabB[sl]], writes=[bbB[sl]])
                    S.dve(f_tt(x3, x3, tab[sl][:, 0:128].unsqueeze(1).broadcast_to([128, 4, 128]), ALU.mult),
                          reads=[xsB[sl], tabB[sl]], writes=[xsB[sl]])
                    S.pool(f_tt(qr[sl][:, :], xs[sl][:, :], bb[sl][:, :], ALU.add),
                           reads=[xsB[sl], bbB[sl]], writes=[qrB[sl]])

                def stage3():
                    pb, pbb = self.next_pb()
                    for c in range(4):
                        S.pe(f_tr(pb[:, c * 128:(c + 1) * 128], qr[sl][:, c * 128:(c + 1) * 128], self.ident_b[:, :]),
                             reads=[qrB[sl], self.identB], writes=[pbb], inc=(c == 3))
                    S.act(f_acopy(tpc[sl][:, :, :], pb[:, 0:512].rearrange("p (a b) -> p a b", a=4)),
                          reads=[pbb], writes=[tpcB[sl]])
                    if kind == "q":
                        S.dma("act", f"st_q{sl}", self.aqT_s[t, :, 4 * gq:4 * gq + 4, :], tpc[sl][:, :, :],
                              reads=[tpcB[sl]], writes=[self.db("aqT", t)])
                    else:
                        S.dma("act", f"st_q{sl}", self.akT_s[t], tpc[sl][:, :, :], reads=[tpcB[sl]], writes=[self.db("akT", t)])
                        if t == NT - 1:
                            S.dma("act", f"st_q{sl}", self.halo_out[:, 0:512], tpc[sl][:, :, :].rearrange("p a b -> p (a b)"),
                                  reads=[tpcB[sl]], writes=[self.db("halo_out")])
                return [stage2, stage3]
            return epi

        def epi_v(t, pf, pfb):
            sl = cnt["e"] % 2
            cnt["e"] += 1
            S.act(f_acopy(vg[sl][:, :], pf[:, :]), reads=[pfb], writes=[vgB[sl]])
            S.dma("act", f"st_v{sl}", self.av_s[t].rearrange("p a b -> p (a b)"), vg[sl][:, :], reads=[vgB[sl]],
                  writes=[self.db("av", t)])
            if t == NT - 1:
                S.dma("act", f"st_v{sl}", self.halo_out[:, 512:1024], vg[sl][:, :], reads=[vgB[sl]],
                      writes=[self.db("halo_out")])

        def epi_g(gq):
            def epi(t, pf, pfb):
                sl = cnt["e"] % 2
                cnt["e"] += 1
                S.act(f_act(vg[sl][:, :], pf[:, :], AF.Silu), reads=[pfb], writes=[vgB[sl]])
                S.dma("act", f"st_v{sl}", self.asg_s[t, :, gq * 512:(gq + 1) * 512], vg[sl][:, :], reads=[vgB[sl]],
                      writes=[self.db("asg", t)])
            return epi

        w = self.att_win[j]
        alltiles = list(range(NT))
        qtiles = alltiles if with_ctx else list(range(2, NT))
        groups = [(w[4], alltiles, epi_qk("k", 0)), (w[5], alltiles, epi_v)]
        for gq in range(4):
            groups.append((w[gq], qtiles, epi_qk("q", gq)))
        for gq in range(4):
            groups.append((w[6 + gq], qtiles, epi_g(gq)))
        k_groups, rest = groups[:2], groups[2:]
        self.proj_phase(k_groups)
        S.op("pool", lambda e: e.collective_compute("AllGather", ALU.bypass, replica_groups=RG,
                                                     ins=[self.halo_out.ap()], outs=[self.halo_in.ap()]),
             reads=[self.db("halo_out")], writes=[self.db("halo_in")], chan="cc_halo", chan_inc=1)
        self.proj_phase(rest)
        self.att_mixer(j, qtiles)
        self.ada_finish()
        self.load_gate(i)
        self.outproj_phase(self.att_wout[j], 16, qtiles)

    def att_mixer(self, j, qtiles):
        S = self.S
        self.U.reset()
        self.M.reset()
        U, M = self.U, self.M
        NB = NT + 1
        kTa, kTaB = U.alloc("kTa", [128, NB, 4, 128], BF16)
        Va, VaB = U.alloc("Va", [128, NB, 4, 130], BF16)
        hst, hstB = M.alloc("hst", [128, 2, 1024], BF16)
        qTt, qTtB = U.allocn("qTt", 2, [128, 16, 128], BF16)
        sgt, sgtB = U.allocn("sgt", 2, [128, D], BF16)
        ut, utB = M.allocn("ut", 2, [128, D], BF16)
        uTs, uTsB = M.alloc("uTs", [128, 16, 128], BF16)
        Et, EtB = M.allocn("Et", 3, [128, 512], BF16)
        dn, dnB = M.allocn("dn", 2, [128, 8], F32)
        S.pool(f_memset(Va[:, :, :, 128:130], 1.0), writes=[VaB])
        for t in range(NT):
            S.dma("sync", "akv", kTa[:, t, :, :], self.akT_s[t], reads=[self.db("akT", t)], writes=[kTaB])
            S.dma("sync", "akv", Va[:, t, :, 0:128], self.av_s[t], reads=[self.db("av", t)], writes=[VaB])
        for r in range(2):
            S.dma("sync", "akv", hst[:, r, :], self.halo_in[r * 128:(r + 1) * 128, :],
                  reads=[self.db("halo_in")], writes=[hstB])
        hk = kTa[:, NT, :, :].rearrange("p a b -> p (a b)")
        S.dve(f_ts(hk, hst[:, 0, 0:512], self.sel[:, 0:1], None, ALU.mult), reads=[hstB, self.constB], writes=[kTaB])
        S.dve(f_stt(hk, hst[:, 1, 0:512], self.sel[:, 1:2], hk, ALU.mult, ALU.add),
              reads=[hstB, self.constB, kTaB], writes=[kTaB])
        hv = Va[:, NT, :, 0:128]
        S.dve(f_ts(hv, hst[:, 0, 512:1024].rearrange("p (a b) -> p a b", a=4), self.sel[:, 0:1], None, ALU.mult),
              reads=[hstB, self.constB], writes=[VaB])
        S.dve(f_stt(hv, hst[:, 1, 512:1024].rearrange("p (a b) -> p a b", a=4), self.sel[:, 1:2], hv, ALU.mult, ALU.add),
              reads=[hstB, self.constB, VaB], writes=[VaB])
        scale = 128.0 ** -0.5
        ei = 0
        for qi, t in enumerate(qtiles):
            sl = qi % 2
            S.dma("sync", f"aq{sl}", qTt[sl][:, :, :], self.aqT_s[t], reads=[self.db("aqT", t)], writes=[qTtB[sl]])
            S.dma("sync", f"aq{sl}", sgt[sl][:, :], self.asg_s[t], reads=[self.db("asg", t)], writes=[sgtB[sl]])
            if t < 2:
                blocks = [(0, None), (1, None)]
            else:
                blocks = []
                if t > 2:
                    blocks.append((t - 1, 0))
                blocks.append((t, None))
                blocks.append((t + 1, 1) if t < NT - 1 else (NT, 2))
                blocks += [(0, None), (1, None)]
            for kvh in range(4):
                accs = [self.next_pf(2, 6), self.next_pf(2, 6)]
                def scores(blk):
                    sc_, scB_ = self.next_pf(0, 2)
                    S.pe(f_mm(sc_[:, :], kTa[:, blk, kvh, :], qTt[sl][:, 4 * kvh:4 * kvh + 4, :].rearrange("p a b -> p (a b)"),
                              True, True), reads=[kTaB, qTtB[sl]], writes=[scB_])
                    return sc_, scB_
                nxt = scores(blocks[0][0])
                for bi, (blk, mk) in enumerate(blocks):
                    sc, scB = nxt
                    if bi + 1 < len(blocks):
                        nxt = scores(blocks[bi + 1][0])
                    e3 = ei % 3
                    ei += 1
                    S.act(f_act(Et[e3][:, :], sc[:, :], AF.Exp, scale=scale), reads=[scB], writes=[EtB[e3]])
                    if mk is not None:
                        ev = Et[e3][:, :].rearrange("p (a b) -> p a b", a=4)
                        S.pool(f_tt(ev, ev, self.amask[:, mk, :].unsqueeze(1).broadcast_to([128, 4, 128]), ALU.mult),
                               reads=[EtB[e3], self.constB], writes=[EtB[e3]])
                    for g in range(4):
                        acc, accB = accs[g // 2]
                        S.pe(f_mm(acc[:, (g % 2) * 130:(g % 2) * 130 + 129], Et[e3][:, g * 128:(g + 1) * 128],
                                  Va[:, blk, kvh, 0:129], bi == 0 and g % 2 == 0, bi == len(blocks) - 1),
                             reads=[EtB[e3], VaB], writes=[accB], inc=(g == 3 or bi == len(blocks) - 1))
                d2 = (qi * 4 + kvh) % 2
                for g in range(4):
                    acc, accB = accs[g // 2]
                    h = 4 * kvh + g
                    o0 = (g % 2) * 130
                    S.dve(f_tt(dn[d2][:, g:g + 1], acc[:, o0 + 128:o0 + 129], self.es[:, h:h + 1], ALU.add),
                          reads=[accB, self.atabB], writes=[dnB[d2]])
                S.dve(f_recip(dn[d2][:, 4:8], dn[d2][:, 0:4]), reads=[dnB[d2]], writes=[dnB[d2]])
                for g in range(4):
                    acc, accB = accs[g // 2]
                    h = 4 * kvh + g
                    o0 = (g % 2) * 130
                    S.dve(f_stt(ut[sl][:, h * 128:(h + 1) * 128], acc[:, o0:o0 + 128], dn[d2][:, 4 + g:5 + g],
                                sgt[sl][:, h * 128:(h + 1) * 128], ALU.mult, ALU.mult),
                          reads=[accB, dnB[d2], sgtB[sl]], writes=[utB[sl]])
            for q4 in range(4):
                pb, pbb = self.next_pb()
                for c in range(4):
                    k = q4 * 4 + c
                    S.pe(f_tr(pb[:, c * 128:(c + 1) * 128], ut[sl][:, k * 128:(k + 1) * 128], self.ident_b[:, :]),
                         reads=[utB[sl], self.identB], writes=[pbb], inc=(c == 3))
                S.act(f_acopy(uTs[:, q4 * 4:q4 * 4 + 4, :], pb[:, 0:512].rearrange("p (a b) -> p a b", a=4)),
                      reads=[pbb], writes=[uTsB])
            S.dma("act", "st_uTa", self.uT_s[0:16, :, t * 128:(t + 1) * 128].rearrange("c f j -> f c j"), uTs[:, :, :],
                  reads=[uTsB], writes=[self.db("uT_s")])
            self.ada_tick()

    def epilogue(self):
        S = self.S
        yb = Buf("y")
        for t in range(2, NT):
            S.dma("sync", "yout", self.y[t - 2], self.x_sb[:, t, :], reads=[self.xB[t]], writes=[yb])
        if self.dbg:
            for t in range(2):
                S.dma("sync", "yout", self.yctx[t], self.x_sb[:, t, :], reads=[self.xB[t]], writes=[yb])
        S.op("sync", None, reads=[yb], inc=False)

    def emit(self):
        nc, S = self.nc, self.S
        with ExitStack() as es:
            sems = {}
            for e in S.ENG:
                sems[("E", e)] = es.enter_context(nc.semaphore(f"p_{e}"))
            for c in S.chans:
                sems[("D", c)] = es.enter_context(nc.semaphore(f"d_{c}"))
            block = es.enter_context(nc.Block())

            def run(name):
                def body(eng):
                    for waits, fn, rec in S.streams[name]:
                        for k, v in waits:
                            eng.wait_ge(sems[k], v)
                        if fn is None:
                            continue
                        ins = fn(eng)
                        if rec is not None:
                            k = rec[0]
                            if k[0] == "D":
                                amt = S.chans[k[1]][1]
                                if amt == 1:
                                    ins.then_inc(sems[k])
                                else:
                                    ins.then_inc(sems[k], amt)
                            else:
                                ins.then_inc(sems[k], 1)
                return body

            block.sync(run("sync"))
            block.scalar(run("act"))
            block.gpsimd(run("pool"))
            block.vector(run("dve"))
            block.tensor(run("pe"))


def _tile_w(w, ncol):
    K, N = w.shape
    nk = K // 128
    a = w.reshape(nk, 128, N // ncol, ncol).transpose(2, 1, 0, 3)
    return np.ascontiguousarray(a).reshape(N // ncol, 128, nk * ncol)


def _rope_tables(pos, is_ctx, half, n2):
    inv = (10000.0 ** (-np.arange(n2, dtype=np.float32) / np.float32(n2))).astype(np.float32)
    row = (pos // 64).astype(np.float32)
    col = (pos % 64).astype(np.float32)
    outc, outs = [], []
    for p in (row, col):
        ang = (p[:, None] * inv[None, :]).astype(np.float32)
        c, s = np.cos(ang).astype(np.float32), np.sin(ang).astype(np.float32)
        outc += [c, c]
        outs += [-s, s]
    cos2 = np.concatenate(outc, axis=1)
    sin2 = np.concatenate(outs, axis=1)
    cos2[is_ctx] = 1.0
    sin2[is_ctx] = 0.0
    return np.concatenate([cos2, sin2], axis=1).astype(np.float32)


def prep_inputs(inp, layers=(0, 1, 2, 3), cores=range(8)):
    f = lambda a: np.ascontiguousarray(np.asarray(a, dtype=np.float32))
    x, c, ctx, c_ctx = f(inp["x"]), f(inp["c"]), f(inp["ctx"]), f(inp["c_ctx"])
    ada_w, ada_b = f(inp["ada_w"]), f(inp["ada_b"])
    shared = {}
    ret_js = sorted({i // 2 for i in layers if i % 2 == 0})
    att_js = sorted({i // 2 for i in layers if i % 2 == 1})
    for j in ret_js:
        shared[f"ret_win{j}"] = _tile_w(f(inp["ret_w_in"][j]), 512)
        shared[f"ret_wout{j}"] = _tile_w(f(inp["ret_w_out"][j]), 256)
    for j in att_js:
        shared[f"att_win{j}"] = _tile_w(f(inp["attn_w_in"][j]), 512)
        shared[f"att_wout{j}"] = _tile_w(f(inp["attn_w_out"][j]), 512)
    ada_half = {}
    for r in range(2):
        for i in layers:
            ada_half[(r, i)] = _tile_w(ada_w[i][:, r * 3072:(r + 1) * 3072], 512)
    ng = f(inp["norm_gain"])
    shared["ng_cols"] = np.ascontiguousarray(ng.reshape(4, 16, 128).transpose(0, 2, 1))
    shared["ret_gain"] = f(inp["ret_norm_gain"])
    shared["att_qg"] = f(inp["attn_q_gain"])
    shared["att_kg"] = f(inp["attn_k_gain"])
    shared["att_sink"] = f(inp["attn_sink"])
    ii = np.arange(128)
    am = np.stack([(ii[:, None] >= ii[None, :]), (ii[:, None] <= ii[None, :]),
                   (ii[:, None] + ii[None, :] >= 127)], axis=1).astype(np.float32)
    shared["amask"] = np.ascontiguousarray(am.reshape(128, 384))
    shared["ident"] = np.eye(128, dtype=np.float32)
    p = np.arange(128, dtype=np.float32)
    ecol = np.stack([-(p + 1), -(127 - p), np.full(128, -128.0), np.zeros(128),
                     -(128 - p), -p, np.full(128, -128.0), np.zeros(128)], axis=1).astype(np.float32)
    shared["ecol"] = ecol
    erow = np.concatenate([p + 1, 128 - p]).astype(np.float32)
    shared["erow"] = np.ascontiguousarray(np.broadcast_to(erow[None, :], (128, 256)))
    lf, lb = f(inp["ret_decay_logit_fwd"]), f(inp["ret_decay_logit_bwd"])
    maps = []
    for core in cores:
        b, r = core // 2, core % 2
        m = dict(shared)
        if r == 0:
            xl, cl = x[b, :1024], ctx[b]
            pos = np.arange(1024)
        else:
            xl, cl = x[b, 1024:][::-1], ctx[b][::-1]
            pos = np.arange(2047, 1023, -1)
        m["x0"] = np.ascontiguousarray(np.concatenate([cl, xl], axis=0)).reshape(NT, 128, D)
        m["cvec"] = np.stack([c[b], c_ctx], axis=0)
        for i in layers:
            m[f"ada_t{i}"] = ada_half[(r, i)]
        m["ada_bh"] = np.ascontiguousarray(ada_b[:, r * 3072:(r + 1) * 3072])
        lg = np.stack([lf, lb], axis=1) if r == 0 else np.stack([lb, lf], axis=1)
        m["ret_logit"] = np.ascontiguousarray(lg.reshape(2, 16))
        allpos = np.concatenate([np.zeros(256, dtype=np.int64), pos])
        is_ctx = np.arange(TOK) < 256
        m["rope_r"] = _rope_tables(allpos, is_ctx, 128, 64).reshape(NT, 128, 512)
        m["rope_a"] = _rope_tables(allpos, is_ctx, 64, 32).reshape(NT, 128, 256)
        incl1, incl2 = (True, False) if r == 0 else (False, True)
        m1 = (ii[None, :] >= ii[:, None]) if incl1 else (ii[None, :] > ii[:, None])
        m2 = (ii[:, None] >= ii[None, :]) if incl2 else (ii[:, None] > ii[None, :])
        m["rmask"] = np.ascontiguousarray(np.stack([m1, m2], axis=1).astype(np.float32).reshape(128, 256))
        sel = np.zeros((128, 2), np.float32)
        sel[:, 1 - r] = 1.0
        m["sel"] = sel
        maps.append(m)
    return maps


_PROG_CACHE = {}


def kernel(**inputs):
    layers = (0, 1, 2, 3)
    if layers not in _PROG_CACHE:
        _PROG_CACHE[layers] = Prog(layers)
    prog = _PROG_CACHE[layers]
    maps = prep_inputs(inputs, layers)
    res = run_bass_kernel_spmd(prog.nc, maps, core_ids=list(range(8)))
    out = np.empty((4, 2048, D), np.float32)
    for core in range(8):
        b, r = core // 2, core % 2
        y = np.asarray(res.results[core]["y"]).reshape(1024, D)
        if r == 0:
            out[b, :1024] = y
        else:
            out[b, 1024:] = y[::-1]
    return out
```
